# Optimizing a Trainium2 kernel written in Bass

```python
import jax, jax.numpy as jnp
from jax import lax
import numpy as np

D_MODEL = 1024
BATCH = 8
SEQ = 4096
DEPTH = 4

GRID_W = 64
CTX_LEN = 256
N_MIXERS = 2
N_HGRN_LAYERS = (DEPTH + 1) // 2
N_MLA_LAYERS = DEPTH // 2
FFN_HIDDEN = 2816
N_MOD = 9
RMS_EPS = 1e-6
HG_HEADS = 8
HG_DK = 128
HG_DV = D_MODEL // HG_HEADS
HG_CHUNK = 32
HG_HK = HG_HEADS * HG_DK
HG_HV = HG_HEADS * HG_DV
HG_IN = 3 * HG_HK + HG_HV + D_MODEL
F_MIN = 1e-6
MLA_HEADS = 8
MLA_NOPE = 128
MLA_ROPE = 64
MLA_V = 128
Q_LORA = 384
KV_LORA = 256
MLA_QK = MLA_NOPE + MLA_ROPE
MLA_DOWN = Q_LORA + KV_LORA + MLA_ROPE
MLA_SCALE = MLA_QK ** -0.5
Q_BLOCK = 128
ROPE_BASE = 10000.0

kernel_name = "hybrid_hgrn2_mla_macaron_prefix_dit"


def rms_norm(x, g):
    xf = x.astype(jnp.float32)
    y = xf * lax.rsqrt(jnp.mean(xf * xf, axis=-1, keepdims=True) + RMS_EPS)
    return (y * g.astype(jnp.float32)).astype(x.dtype)


def modulate(h, shift, scale):
    return h * (1 + scale) + shift


def swiglu(h, w_gate, w_up, w_down):
    return (jax.nn.silu(h @ w_gate) * (h @ w_up)) @ w_down


def ffn_half(h, mod, base, g, w_gate, w_up, w_down):
    a = modulate(rms_norm(h, g), mod[:, base], mod[:, base + 1])
    return h + 0.5 * mod[:, base + 2] * swiglu(a, w_gate, w_up, w_down)


def axial_rope_tables(n_tokens):
    rows = n_tokens // GRID_W
    row = jnp.repeat(jnp.arange(rows), GRID_W).astype(jnp.float32)
    col = jnp.tile(jnp.arange(GRID_W), rows).astype(jnp.float32)
    axis_dims = MLA_ROPE // 2
    inv = 1.0 / (ROPE_BASE ** (jnp.arange(0, axis_dims, 2, dtype=jnp.float32) / axis_dims))
    ang = jnp.stack([row[:, None] * inv, col[:, None] * inv], axis=1)
    return jnp.cos(ang), jnp.sin(ang)


def apply_axial_rope(x, cos, sin):
    xr = x.reshape(x.shape[:-1] + (2, 2, MLA_ROPE // 4))
    a, b = xr[..., 0, :], xr[..., 1, :]
    cos = cos.astype(x.dtype)
    sin = sin.astype(x.dtype)
    out = jnp.stack([a * cos - b * sin, a * sin + b * cos], axis=-2)
    return out.reshape(x.shape)


def _heads(a, d):
    b_, t_, _ = a.shape
    return a.reshape(b_, t_, -1, d).transpose(0, 2, 1, 3)


def hgrn2_chunk_scan(q, k, v, logf, s0, with_output):
    b_, h_, t_, dk = q.shape
    n = t_ // HG_CHUNK

    def chunks(a):
        return jnp.moveaxis(a.reshape(b_, h_, n, HG_CHUNK, a.shape[-1]), 2, 0)

    causal = jnp.tril(jnp.ones((HG_CHUNK, HG_CHUNK), dtype=bool))[:, :, None]
    causal_f = causal.astype(jnp.float32)

    def step(s, inp):
        qb, kb, vb, gb = inp
        cum = jnp.cumsum(gb, axis=2)
        cum_last = cum[:, :, -1:, :]
        k_end = kb * jnp.exp(cum_last - cum)
        s_new = jnp.exp(cum_last)[:, :, 0, :, None] * s + jnp.einsum('bhck,bhcv->bhkv', k_end, vb)
        if not with_output:
            return s_new, None
        diff = cum[:, :, :, None, :] - cum[:, :, None, :, :]
        decay = jnp.exp(jnp.where(causal, diff, 0.0)) * causal_f
        scores = jnp.einsum('bhtk,bhtsk,bhsk->bhts', qb, decay, kb)
        o = jnp.einsum('bhtk,bhkv->bhtv', qb * jnp.exp(cum), s) + jnp.einsum('bhts,bhsv->bhtv', scores, vb)
        return s_new, o

    s_fin, o = lax.scan(step, s0, (chunks(q), chunks(k), chunks(v), chunks(logf)))
    if with_output:
        o = jnp.moveaxis(o, 0, 2).reshape(b_, h_, t_, v.shape[-1])
    return s_fin, o


def hgrn2_inputs(a, w_in, lb):
    p = a @ w_in
    q, z_f, z_b, i, g = jnp.split(p, [HG_HK, 2 * HG_HK, 3 * HG_HK, 3 * HG_HK + HG_HV], axis=-1)
    q = _heads(q, HG_DK).astype(jnp.float32)
    v = _heads(i, HG_DV).astype(jnp.float32)
    dirs = []
    for d, z in enumerate((z_f, z_b)):
        zf = _heads(z, HG_DK).astype(jnp.float32)
        lbd = lb[d].reshape(HG_HEADS, 1, HG_DK)
        f = lbd + (1 - lbd) * jax.nn.sigmoid(zf)
        logf = jnp.log(jnp.maximum(f, F_MIN))
        k = (1 - lbd) * jax.nn.sigmoid(-zf)
        dirs.append((k, logf))
    return q, v, dirs[0], dirs[1], g


def hgrn2_mixer(u, uc, w_in, w_out, gn, lb, with_ctx_out):
    q, v, (kf, gf), (kb, gb), g = hgrn2_inputs(u, w_in, lb)
    qc, vc, (kcf, gcf), (kcb, gcb), gc = hgrn2_inputs(uc, w_in, lb)
    zero = jnp.zeros((u.shape[0], HG_HEADS, HG_DK, HG_DV), jnp.float32)
    flip = lambda a: jnp.flip(a, axis=2)
    s_cf, o_cf = hgrn2_chunk_scan(qc, kcf, vc, gcf, zero, with_ctx_out)
    s_cb, o_cb = hgrn2_chunk_scan(flip(qc), flip(kcb), flip(vc), flip(gcb), zero, with_ctx_out)
    _, o_f = hgrn2_chunk_scan(q, kf, v, gf, s_cf, True)
    _, o_b = hgrn2_chunk_scan(flip(q), flip(kb), flip(v), flip(gb), s_cb, True)

    def readout(o, gate):
        o = rms_norm(o, gn)
        b_, h_, t_, dv = o.shape
        o = o.transpose(0, 2, 1, 3).reshape(b_, t_, h_ * dv).astype(gate.dtype)
        return (o * jax.nn.silu(gate)) @ w_out

    y = readout(o_f + flip(o_b), g)
    yc = readout(o_cf + flip(o_cb), gc) if with_ctx_out else None
    return y, yc


def mla_project(a, w_down, q_norm, w_uq, kv_norm, w_ukv, rope):
    b_, t_, _ = a.shape
    dproj = a @ w_down
    cq, ckv, kr = jnp.split(dproj, [Q_LORA, Q_LORA + KV_LORA], axis=-1)
    q = (rms_norm(cq, q_norm) @ w_uq).reshape(b_, t_, MLA_HEADS, MLA_QK)
    kv = (rms_norm(ckv, kv_norm) @ w_ukv).reshape(b_, t_, MLA_HEADS, MLA_NOPE + MLA_V)
    q_nope, q_rope = jnp.split(q, [MLA_NOPE], axis=-1)
    k_nope, v = jnp.split(kv, [MLA_NOPE], axis=-1)
    if rope is not None:
        cos, sin = rope
        q_rope = apply_axial_rope(q_rope, cos[:, None], sin[:, None])
        kr = apply_axial_rope(kr, cos, sin)
    k_rope = jnp.broadcast_to(kr[:, :, None, :], (b_, t_, MLA_HEADS, MLA_ROPE))
    q = jnp.concatenate([q_nope, q_rope], axis=-1)
    k = jnp.concatenate([k_nope, k_rope], axis=-1)
    return q, k, v


def attend(q, k, v):
    s = jnp.einsum('bqhd,bkhd->bhqk', q, k).astype(jnp.float32) * MLA_SCALE
    p = jax.nn.softmax(s, axis=-1).astype(v.dtype)
    return jnp.einsum('bhqk,bkhd->bqhd', p, v)


def mla_mixer(u, uc, w_down, q_norm, w_uq, kv_norm, w_ukv, w_o, cos, sin, with_ctx_out):
    b_, t_, _ = u.shape
    q, k, v = mla_project(u, w_down, q_norm, w_uq, kv_norm, w_ukv, (cos, sin))
    qc, kc, vc = mla_project(uc, w_down, q_norm, w_uq, kv_norm, w_ukv, None)
    k_all = jnp.concatenate([k, kc], axis=1)
    v_all = jnp.concatenate([v, vc], axis=1)
    nb = t_ // Q_BLOCK
    qb = jnp.moveaxis(q.reshape(b_, nb, Q_BLOCK, MLA_HEADS, MLA_QK), 1, 0)
    ob = lax.map(lambda blk: attend(blk, k_all, v_all), qb)
    o = jnp.moveaxis(ob, 0, 1).reshape(b_, t_, MLA_HEADS * MLA_V)
    y = o @ w_o
    yc = attend(qc, kc, vc).reshape(b_, uc.shape[1], MLA_HEADS * MLA_V) @ w_o if with_ctx_out else None
    return y, yc


def setup_inputs(seed: int = 0) -> dict:
    key = jax.random.key(seed)
    ks = jax.random.split(key, 24)
    nrm = lambda k, shape, s: jax.random.normal(k, shape, jnp.float32) * s
    D = D_MODEL
    return {
        'x': nrm(ks[0], (BATCH, SEQ, D), 1.0),
        'c': nrm(ks[1], (BATCH, D), 1.0),
        'ctx': nrm(ks[2], (BATCH, CTX_LEN, D), 1.0),
        'c_ctx': nrm(ks[3], (D,), 1.0),
        'mod_w': nrm(ks[4], (DEPTH, D, N_MOD * D), 0.5 * D ** -0.5),
        'mod_b': nrm(ks[5], (DEPTH, N_MOD * D), 0.01),
        'norm_g': 1.0 + nrm(ks[6], (DEPTH, 3, D), 0.02),
        'ffn_w_gate': nrm(ks[7], (DEPTH, 2, D, FFN_HIDDEN), D ** -0.5),
        'ffn_w_up': nrm(ks[8], (DEPTH, 2, D, FFN_HIDDEN), D ** -0.5),
        'ffn_w_down': nrm(ks[9], (DEPTH, 2, FFN_HIDDEN, D), FFN_HIDDEN ** -0.5),
        'hg_w_in': nrm(ks[10], (N_HGRN_LAYERS, D, HG_IN), D ** -0.5),
        'hg_w_out': nrm(ks[11], (N_HGRN_LAYERS, HG_HV, D), HG_HV ** -0.5),
        'hg_gn': 1.0 + nrm(ks[12], (N_HGRN_LAYERS, HG_DV), 0.02),
        'hg_lb_logits': nrm(ks[13], (N_HGRN_LAYERS, 2, HG_HK), 0.5),
        'mla_w_down': nrm(ks[14], (N_MLA_LAYERS, D, MLA_DOWN), D ** -0.5),
        'mla_q_norm': 1.0 + nrm(ks[15], (N_MLA_LAYERS, Q_LORA), 0.02),
        'mla_w_uq': nrm(ks[16], (N_MLA_LAYERS, Q_LORA, MLA_HEADS * MLA_QK), Q_LORA ** -0.5),
        'mla_kv_norm': 1.0 + nrm(ks[17], (N_MLA_LAYERS, KV_LORA), 0.02),
        'mla_w_ukv': nrm(ks[18], (N_MLA_LAYERS, KV_LORA, MLA_HEADS * (MLA_NOPE + MLA_V)), KV_LORA ** -0.5),
        'mla_w_o': nrm(ks[19], (N_MLA_LAYERS, MLA_HEADS * MLA_V, D), (MLA_HEADS * MLA_V) ** -0.5),
        'final_g': 1.0 + nrm(ks[20], (D,), 0.02),
    }


def reference(x, c, ctx, c_ctx, mod_w, mod_b, norm_g, ffn_w_gate, ffn_w_up, ffn_w_down,
              hg_w_in, hg_w_out, hg_gn, hg_lb_logits, mla_w_down, mla_q_norm, mla_w_uq,
              mla_kv_norm, mla_w_ukv, mla_w_o, final_g):
    n_tok = x.shape[1]
    cos, sin = axial_rope_tables(n_tok)
    p_lb = jax.nn.softmax(hg_lb_logits.astype(jnp.float32), axis=0)
    lbs = jnp.maximum(jnp.cumsum(p_lb, axis=0) - p_lb[0:1], 0.0)
    sc = jax.nn.silu(c)
    scc = jax.nn.silu(c_ctx)[None]
    h, hc = x, ctx
    for i in range(DEPTH):
        last = i == DEPTH - 1
        mod = (sc @ mod_w[i] + mod_b[i]).reshape(-1, N_MOD, 1, D_MODEL)
        modc = (scc @ mod_w[i] + mod_b[i]).reshape(-1, N_MOD, 1, D_MODEL)
        h = ffn_half(h, mod, 0, norm_g[i, 0], ffn_w_gate[i, 0], ffn_w_up[i, 0], ffn_w_down[i, 0])
        hc = ffn_half(hc, modc, 0, norm_g[i, 0], ffn_w_gate[i, 0], ffn_w_up[i, 0], ffn_w_down[i, 0])
        u = modulate(rms_norm(h, norm_g[i, 1]), mod[:, 3], mod[:, 4])
        uc = modulate(rms_norm(hc, norm_g[i, 1]), modc[:, 3], modc[:, 4])
        j = i // N_MIXERS
        if i % N_MIXERS == 0:
            y, yc = hgrn2_mixer(u, uc, hg_w_in[j], hg_w_out[j], hg_gn[j], lbs[j], not last)
        else:
            y, yc = mla_mixer(u, uc, mla_w_down[j], mla_q_norm[j], mla_w_uq[j], mla_kv_norm[j],
                              mla_w_ukv[j], mla_w_o[j], cos, sin, not last)
        h = h + mod[:, 5] * y
        h = ffn_half(h, mod, 6, norm_g[i, 2], ffn_w_gate[i, 1], ffn_w_up[i, 1], ffn_w_down[i, 1])
        if not last:
            hc = hc + modc[:, 5] * yc
            hc = ffn_half(hc, modc, 6, norm_g[i, 2], ffn_w_gate[i, 1], ffn_w_up[i, 1], ffn_w_down[i, 1])
    return rms_norm(h, final_g)
```

```python
import numpy as np
from contextlib import ExitStack
import concourse.bass as bass
import concourse.mybir as mybir
from concourse.bass_utils import run_bass_kernel_spmd

F32 = mybir.dt.float32
BF16 = mybir.dt.bfloat16
AF = mybir.ActivationFunctionType
ALU = mybir.AluOpType

D = 1024
NCH = 8
CTX = 256
NFC = 22
NMOD = 9
RMS_EPS = 1e-6
F_MIN = 1e-6
MLA_SCALE = 192 ** -0.5

ENGS = ('pe', 'act', 'dve', 'pool', 'sp')
DMAQ = ('sp', 'act', 'pool')
NSLOT = 10
SEMCAP = 30000


class Buf:
    __slots__ = ('w', 'r', 'rd')

    def __init__(self):
        self.w = None
        self.r = {}
        self.rd = []


class Op:
    __slots__ = ('eng', 'fn', 'deps', 'dma', 'sig', 'semi', 'val', 'prev', 'slot', 'idx')


class Sched:
    def __init__(self):
        self.ops = {e: [] for e in ENGS}
        self.bar = {e: [] for e in ENGS}
        self.dma_since = []

    def add(self, eng, fn, reads=(), writes=(), dma=False):
        op = Op()
        op.eng, op.fn, op.dma, op.sig = eng, fn, dma, False
        op.idx = len(self.ops[eng])
        deps = {}

        def flat(bs):
            out = []
            for b in bs:
                if isinstance(b, (tuple, list)):
                    out.extend(b)
                else:
                    out.append(b)
            return out
        reads = flat(reads)
        writes = flat(writes)

        def need(d, raw):
            if d is None or d is op:
                return
            if d.dma:
                deps[id(d)] = d
                return
            if (not dma) and d.eng == eng:
                if eng == 'pe':
                    return
            k = d.eng
            if k not in deps or deps[k].idx < d.idx:
                deps[k] = d

        for b in reads:
            need(b.w, True)
        for b in writes:
            need(b.w, False)
            for r in b.r.values():
                need(r, False)
            for r in b.rd:
                need(r, False)
        for d in self.bar[eng]:
            need(d, True)
        self.bar[eng] = []
        for b in reads:
            if dma:
                b.rd.append(op)
            else:
                b.r[eng] = op
        for b in writes:
            b.w = op
            b.r = {}
            b.rd = []
        op.deps = list(deps.values())
        for d in op.deps:
            d.sig = True
        self.ops[eng].append(op)
        if dma:
            self.dma_since.append(op)
        return op

    def barrier(self):
        tails = [self.ops[e][-1] for e in ENGS if self.ops[e] and not self.ops[e][-1].dma]
        tails = []
        for e in ENGS:
            for op in reversed(self.ops[e]):
                if not op.dma:
                    tails.append(op)
                    break
        tails += self.dma_since
        self.dma_since = []
        for e in ENGS:
            self.bar[e] = self.bar[e] + tails

    def finalize(self):
        self.nsem = {}
        for e in ('pe', 'act', 'dve', 'pool'):
            cnt = 0
            for op in self.ops[e]:
                if op.dma:
                    continue
                if op.sig:
                    op.semi = cnt // SEMCAP
                    op.val = cnt % SEMCAP + 1
                    cnt += 1
            self.nsem[e] = cnt // SEMCAP + 1
        self.dma_final = {}
        for q in DMAQ:
            cnt = [0] * NSLOT
            j = 0
            for op in self.ops[q]:
                if not op.dma:
                    continue
                sl = j % NSLOT
                op.slot = sl
                op.prev = cnt[sl] * 16
                cnt[sl] += 1
                op.val = cnt[sl] * 16
                j += 1
            self.dma_final[q] = [c * 16 for c in cnt]

    def emit_engine(self, e, eng, csem, dsem):
        known = {}

        def wait(key, sem, val):
            if known.get(key, 0) < val:
                eng.wait_ge(sem, val)
                known[key] = val

        for op in self.ops[e]:
            for d in op.deps:
                if d.dma:
                    wait(('d', d.eng, d.slot), dsem[d.eng][d.slot], d.val)
                else:
                    wait(('c', d.eng, d.semi), csem[d.eng][d.semi], d.val)
            if op.dma and op.prev > 0:
                wait(('d', e, op.slot), dsem[e][op.slot], op.prev)
            ins = op.fn(eng)
            if op.dma:
                ins.then_inc(dsem[e][op.slot], 16)
            elif op.sig:
                ins.then_inc(csem[e][op.semi], 1)
        if e == 'sp':
            for q in DMAQ:
                for sl in range(NSLOT):
                    if self.dma_final[q][sl] > 0:
                        wait(('d', q, sl), dsem[q][sl], self.dma_final[q][sl])


class T:
    __slots__ = ('ap', 'buf')

    def __init__(self, ap, buf=None):
        self.ap = ap
        self.buf = buf if buf is not None else Buf()

    def __getitem__(self, k):
        return T(self.ap[k], self.buf)

    def re(self, pat, **kw):
        return T(self.ap.rearrange(pat, **kw), self.buf)

    def bc(self, shape):
        return T(self.ap.to_broadcast(shape), self.buf)


class Arena:
    def __init__(self, ap):
        self.ap = ap
        self.n = ap.shape[1]
        self.off = 0

    def reset(self):
        self.off = 0

    def f32(self, n):
        a = self.ap[:, self.off:self.off + n]
        self.off += n
        assert self.off <= self.n, "arena overflow %d > %d" % (self.off, self.n)
        return T(a)

    def bf16(self, n):
        w = (n + 1) // 2
        a = self.ap[:, self.off:self.off + w].bitcast(BF16)
        self.off += w
        assert self.off <= self.n, "arena overflow %d > %d" % (self.off, self.n)
        return T(a[:, 0:n])


COL_C = 0
COL_MODB = COL_C + 16
COL_NG = COL_MODB + 4 * 72
COL_FG = COL_NG + 96
COL_GN = COL_FG + 8
COL_LB = COL_GN + 2
COL_QN = COL_LB + 32
COL_KVN = COL_QN + 6
NCOLS = COL_KVN + 4


class K:
    def __init__(self, cfg):
        self.cfg = cfg
        self.TL = cfg.get('TL', 4096)
        self.T = self.TL + CTX
        self.depth = cfg.get('depth', 4)
        self.dump = cfg.get('dump', ())
        self.nc = bass.Bass("TRN2", target_bir_lowering=False)
        self.S = Sched()

    def din(self, name, shape, dt=F32):
        return self.nc.dram_tensor(name, list(shape), dt, kind="ExternalInput").ap()

    def dscr(self, name, shape, dt):
        kind = "ExternalOutput" if name in self.dump else "Internal"
        return self.nc.dram_tensor(name, list(shape), dt, kind=kind).ap()

    def mm(self, out, lhsT, rhs, start=True, stop=True):
        o, l, r = out.ap, lhsT.ap, rhs.ap
        self.S.add('pe', lambda e: e.matmul(o, lhsT=l, rhs=r, start=start, stop=stop),
                   [lhsT.buf, rhs.buf], [out.buf])

    def tr(self, out, in_, ident):
        o, i, d = out.ap, in_.ap, ident.ap
        self.S.add('pe', lambda e: e.transpose(o, i, d), [in_.buf, ident.buf], [out.buf])

    def act(self, out, in_, func, scale=1.0, bias=None):
        reads = [in_.buf]
        sc, bi = scale, bias
        if isinstance(scale, T):
            reads.append(scale.buf)
            sc = scale.ap
        if isinstance(bias, T):
            reads.append(bias.buf)
            bi = bias.ap
        o, i = out.ap, in_.ap
        if bi is None:
            self.S.add('act', lambda e: e.activation(out=o, in_=i, func=func, scale=sc), reads, [out.buf])
        else:
            self.S.add('act', lambda e: e.activation(out=o, in_=i, func=func, scale=sc, bias=bi), reads, [out.buf])

    def tt(self, eng, out, a, b, op):
        o, x, y = out.ap, a.ap, b.ap
        self.S.add(eng, lambda e: e.tensor_tensor(out=o, in0=x, in1=y, op=op), [a.buf, b.buf], [out.buf])

    def ts(self, eng, out, a, s1, s2, op0, op1=None):
        reads = [a.buf]
        v1, v2 = s1, s2
        if isinstance(s1, T):
            reads.append(s1.buf)
            v1 = s1.ap
        if isinstance(s2, T):
            reads.append(s2.buf)
            v2 = s2.ap
        o, x = out.ap, a.ap
        if op1 is None:
            self.S.add(eng, lambda e: e.tensor_scalar(out=o, in0=x, scalar1=v1, scalar2=None, op0=op0),
                       reads, [out.buf])
        else:
            self.S.add(eng, lambda e: e.tensor_scalar(out=o, in0=x, scalar1=v1, scalar2=v2, op0=op0, op1=op1),
                       reads, [out.buf])

    def stt(self, out, in0, scalar, in1, op0, op1):
        reads = [in0.buf, in1.buf]
        sc = scalar
        if isinstance(scalar, T):
            reads.append(scalar.buf)
            sc = scalar.ap
        o, x, y = out.ap, in0.ap, in1.ap
        self.S.add('dve', lambda e: e.scalar_tensor_tensor(out=o, in0=x, scalar=sc, in1=y, op0=op0, op1=op1),
                   reads, [out.buf])

    def copy(self, eng, out, in_):
        o, i = out.ap, in_.ap
        if eng == 'act':
            self.S.add('act', lambda e: e.copy(out=o, in_=i), [in_.buf], [out.buf])
        else:
            self.S.add(eng, lambda e: e.tensor_copy(out=o, in_=i), [in_.buf], [out.buf])

    def recip(self, out, in_):
        o, i = out.ap, in_.ap
        self.S.add('dve', lambda e: e.reciprocal(out=o, in_=i), [in_.buf], [out.buf])

    def memset(self, eng, out, val):
        o = out.ap
        self.S.add(eng, lambda e: e.memset(o, val), [], [out.buf])

    def scan(self, out, d0, d1, init, op0, op1):
        o, a, b = out.ap, d0.ap, d1.ap
        self.S.add('dve', lambda e: e.tensor_tensor_scan(out=o, data0=a, data1=b, initial=init, op0=op0, op1=op1),
                   [d0.buf, d1.buf], [out.buf])

    def dma_in(self, q, dst, src_ap, **kw):
        o = dst.ap
        self.S.add(q, lambda e: e.dma_start(out=o, in_=src_ap, **kw), [], [dst.buf], dma=True)

    def dma_out(self, q, dst_ap, src, **kw):
        i = src.ap
        self.S.add(q, lambda e: e.dma_start(out=dst_ap, in_=i, **kw), [src.buf], [], dma=True)

    def wload(self, dst, src_ap):
        self.dma_in('pool', dst, src_ap, max_dma_last_dim=4096)

    def phase_end(self):
        self.S.barrier()
        self.arena.reset()

    def tiles(self, with_ctx=True):
        out = [(0, CTX, 1)] if with_ctx else []
        out += [(CTX + 512 * i, 512, 0) for i in range(self.TL // 512)]
        return out

    def col(self, c):
        return self.cols[:, c:c + 1]

    def modc(self, i, m, ch, w):
        c = ((i * NMOD + m) * NCH + ch) * 2 + w
        return self.MODC[:, c:c + 1]

    def gmc(self, i, s_, ch, w):
        c = ((i * 3 + s_) * NCH + ch) * 2 + w
        return self.GM[:, c:c + 1]

    def hgc(self, i, s_, ch, w):
        c = ((i * 3 + s_) * NCH + ch) * 2 + w
        return self.HG[:, c:c + 1]

    def phase_consts(self):
        s = self
        s.dma_in('sp', s.cols, s.d_cols)
        s.dma_in('sp', s.identf, s.d_consts[:, 0:128])
        cf = s.arena.f32(384)
        s.dma_in('sp', cf, s.d_consts)
        s.copy('dve', s.cst, cf)
        s.memset('pool', s.ones, 1.0)
        s.memset('pool', s.LB[:, 0:16], 0.0)
        dl = s.arena.f32(16)
        s.tt('dve', dl, s.cols[:, COL_LB + 16:COL_LB + 32], s.cols[:, COL_LB:COL_LB + 16], ALU.subtract)
        s.act(s.LB[:, 16:32], dl, AF.Sigmoid)
        s.ts('dve', s.OML, s.LB, -1.0, 1.0, ALU.mult, ALU.add)
        s.phase_end()

    def mod_derive(self, i):
        s = self
        for s_ in range(3):
            for w in range(2):
                b_sc = ((i * NMOD + 3 * s_ + 1) * NCH) * 2
                b_gt = ((i * NMOD + 3 * s_ + 2) * NCH) * 2
                b_o = ((i * 3 + s_) * NCH) * 2
                ng = s.cols[:, COL_NG + (i * 3 + s_) * 8:COL_NG + (i * 3 + s_) * 8 + 8]
                s.stt(s.GM[:, b_o + w:b_o + 16:2], s.MODC[:, b_sc + w:b_sc + 16:2], 1.0, ng,
                      ALU.add, ALU.mult)
                s.ts('dve', s.HG[:, b_o + w:b_o + 16:2], s.MODC[:, b_gt + w:b_gt + 16:2],
                     0.5 if s_ != 1 else 1.0, None, ALU.mult)

    def mod_bg_steps(self, layers, A):
        s = self
        Wb = [[A.bf16(1024) for _ in range(8)] for _ in range(2)]
        rowb = A.f32(1024)
        blocks = [(i, m) for i in layers for m in range(NMOD)]
        steps = []

        def stepA(bi):
            i, m = blocks[bi]
            for k in range(8):
                s.wload(Wb[bi % 2][k], s.d_mod_w[i, k * 128:(k + 1) * 128, m * 1024:(m + 1) * 1024])

        def stepB(bi):
            i, m = blocks[bi]
            W = Wb[bi % 2]
            for half in range(2):
                pb = s.ps[6 + half]
                for k in range(8):
                    s.mm(pb[0:2, :], s.scbf[:, k, :], W[k][:, half * 512:(half + 1) * 512], k == 0, k == 7)
                s.copy('act', rowb[0:2, half * 512:(half + 1) * 512], pb[0:2, :])
            pt = s.ps[6]
            for ch in range(8):
                s.mm(pt[:, ch * 2:ch * 2 + 2], rowb[0:2, ch * 128:(ch + 1) * 128], s.identf[0:2, 0:2])
            base = ((i * NMOD + m) * NCH) * 2
            for w in range(2):
                s.tt('dve', s.MODC[:, base + w:base + 16:2], pt[:, w:16:2],
                     s.cols[:, COL_MODB + i * 72 + m * 8:COL_MODB + i * 72 + m * 8 + 8], ALU.add)
            if m == NMOD - 1:
                s.mod_derive(i)

        for bi in range(len(blocks)):
            def st(bi=bi):
                if bi == 0:
                    stepA(0)
                if bi + 1 < len(blocks):
                    stepA(bi + 1)
                stepB(bi)
            steps.append(st)
        return steps

    def phase_mod(self, layers):
        s = self
        A = s.arena
        sc = A.f32(16)
        s.act(sc, s.cols[:, COL_C:COL_C + 16], AF.Silu)
        sc3 = sc.re("p (w c) -> p c w", w=2)
        s.copy('dve', s.scbf, sc3)
        wb = [[T(A.f32(1024).ap) for _ in range(8)] for _ in range(2)]
        blk = 0
        for i in layers:
            for m in range(NMOD):
                W = wb[blk % 2]
                for k in range(8):
                    s.dma_in('sp' if k % 2 == 0 else 'act', W[k],
                             s.d_mod_w[i, k * 128:(k + 1) * 128, m * 1024:(m + 1) * 1024])
                ps = s.ps[blk % 2]
                for ch in range(8):
                    for k in range(8):
                        s.mm(ps[:, ch * 2:ch * 2 + 2], W[k][:, ch * 128:(ch + 1) * 128], sc3[:, k, :],
                             k == 0, k == 7)
                base = ((i * NMOD + m) * NCH) * 2
                for w in range(2):
                    outv = s.MODC[:, base + w:base + 16:2]
                    s.tt('dve', outv, ps[:, w:16:2],
                         s.cols[:, COL_MODB + i * 72 + m * 8:COL_MODB + i * 72 + m * 8 + 8], ALU.add)
                blk += 1
            s.mod_derive(i)
        s.phase_end()

    def norm_mod(self, h3, a3, n, gm, sh, sq, rt, rstd, tmps, D_=D, nch=NCH, out_scale=None, stage=0):
        s = self
        if stage in (0, 1):
            s.act(sq[:, :, :n], h3[:, :, :n], AF.Square)
        if stage == 1:
            return
        pst = s.ps[6]
        if stage in (0, 2):
            for ch in range(nch):
                s.mm(pst[:, :n], s.ones, sq[:, ch, :n], ch == 0, ch == nch - 1)
            s.act(rt[:, :n], pst[:, :n], AF.Sqrt, scale=1.0 / D_, bias=s.epsc)
            s.recip(rstd[:, :n], rt[:, :n])
        if stage == 2:
            return
        for ch in (range(nch) if stage == 0 else [stage - 3]):
            tm = tmps[ch % 2]
            s.tt('dve' if ch % 2 == 0 else 'pool', tm[:, :n], h3[:, ch, :n], rstd[:, :n], ALU.mult)
            if sh is None:
                s.act(a3[:, ch, :n], tm[:, :n], AF.Copy, scale=gm(ch))
            else:
                s.act(a3[:, ch, :n], tm[:, :n], AF.Identity, scale=gm(ch), bias=sh(ch))

    def hview(self, dram, t0, n):
        return dram.rearrange("(c p) t -> p c t", p=128)[:, :, t0:t0 + n]

    def ffn_wsets(self):
        A = self.arena
        if not hasattr(self, '_wsets'):
            assert A.off == 0
            self._wsets = []
            for b in range(2):
                Wg = [A.bf16(1024) for _ in range(8)]
                Wu = [A.bf16(1024) for _ in range(8)]
                Wd = [A.bf16(1024) for _ in range(8)]
                self._wsets.append((Wg, Wu, Wd))
            self._wset_end = A.off
        A.off = self._wset_end
        return self._wsets

    def ffn_wload(self, desc, setidx):
        s = self
        (i, j, fa, fb) = desc[:4]
        nf = fb - fa
        Wg, Wu, Wd = s._wsets[setidx]
        for k in range(8):
            s.wload(Wg[k][:, :nf * 128], s.d_wg[i, j, k * 128:(k + 1) * 128, fa * 128:fb * 128])
            s.wload(Wu[k][:, :nf * 128], s.d_wu[i, j, k * 128:(k + 1) * 128, fa * 128:fb * 128])
        for f in range(nf):
            s.wload(Wd[f], s.d_wd[i, j, (fa + f) * 128:(fa + f + 1) * 128, :])

    def ffn_pass(self, desc, pidx, prefetched, nxt):
        s = self
        A = s.arena
        (i, j, fa, fb, first, src, dst, with_ctx) = desc
        nf = fb - fa
        s_ = 0 if j == 0 else 2
        Wg, Wu, Wd = s.ffn_wsets()[pidx % 2]
        if not prefetched:
            s.ffn_wload(desc, pidx % 2)
        if nxt is not None:
            s.ffn_wload(nxt, (pidx + 1) % 2)
        hb = [A.f32(8 * 512).re("p (c t) -> p c t", c=8) for _ in range(2)]
        ab = [A.bf16(8 * 512).re("p (c t) -> p c t", c=8) for _ in range(2)]
        hm = [[A.bf16(512) for _ in range(nf)] for _ in range(2)]
        sg = [A.f32(512) for _ in range(2)]
        if first:
            sq = A.bf16(8 * 512).re("p (c t) -> p c t", c=8)
            rt = A.f32(512)
            rstd = A.f32(512)
            tmps = [A.f32(512) for _ in range(2)]
        tiles = s.tiles(with_ctx)

        def load(ti):
            t0, n, w = tiles[ti]
            s.dma_in('sp', hb[ti % 2][:, :, :n], s.hview(src if first else dst, t0, n))
            if not first:
                s.dma_in('sp', ab[ti % 2][:, :, :n], s.hview(s.d_AT, t0, n))

        def norm(ti, stage):
            t0, n, w = tiles[ti]
            s.norm_mod(hb[ti % 2], ab[ti % 2], n, lambda ch: s.gmc(i, s_, ch, w),
                       lambda ch: s.modc(i, 3 * s_, ch, w), sq, rt, rstd, tmps, stage=stage)
            if stage in (0, 10):
                s.dma_out('sp', s.hview(s.d_AT, t0, n), ab[ti % 2][:, :, :n])

        load(0)
        if first:
            norm(0, 0)
        for ti, (t0, n, w) in enumerate(tiles):
            if ti + 1 < len(tiles):
                load(ti + 1)
            b = ti % 2
            h3, a3 = hb[b], ab[b]
            for f in range(nf):
                if first and ti + 1 < len(tiles):
                    if f == nf // 2:
                        norm(ti + 1, 1)
                    if f == nf // 2 + 2:
                        norm(ti + 1, 2)
                pg, pu = s.ps[f % 2], s.ps[2 + f % 2]
                for k in range(8):
                    s.mm(pg[:, :n], Wg[k][:, f * 128:(f + 1) * 128], a3[:, k, :n], k == 0, k == 7)
                for k in range(8):
                    s.mm(pu[:, :n], Wu[k][:, f * 128:(f + 1) * 128], a3[:, k, :n], k == 0, k == 7)
                s.act(sg[f % 2][:, :n], pg[:, :n], AF.Silu)
                s.tt('dve', hm[b][f][:, :n], sg[f % 2][:, :n], pu[:, :n], ALU.mult)
            for dch in range(8):
                if first and ti + 1 < len(tiles):
                    norm(ti + 1, 3 + dch)
                py = s.ps[4 + dch % 2]
                for f in range(nf):
                    s.mm(py[:, :n], Wd[f][:, dch * 128:(dch + 1) * 128], hm[b][f][:, :n], f == 0, f == nf - 1)
                s.stt(h3[:, dch, :n], py[:, :n], s.hgc(i, s_, dch, w), h3[:, dch, :n], ALU.mult, ALU.add)
            s.dma_out('sp', s.hview(dst, t0, n), h3[:, :, :n])
        s.phase_end()

    def ffn_descs(self, i, j, src, dst, with_ctx=True):
        splits = self.cfg.get('fsplits', [(0, 8), (8, 15), (15, 22)])
        return [('ffn', (i, j, fa, fb, si == 0, src, dst, with_ctx)) for si, (fa, fb) in enumerate(splits)]

    def phase_final(self, src):
        s = self
        A = s.arena
        hb = [A.f32(8 * 512).re("p (c t) -> p c t", c=8) for _ in range(2)]
        ob = [A.f32(8 * 512).re("p (c t) -> p c t", c=8) for _ in range(2)]
        sq = A.bf16(8 * 512).re("p (c t) -> p c t", c=8)
        rt = A.f32(512)
        rstd = A.f32(512)
        tiles = s.tiles(False)
        s.dma_in('sp', hb[0], s.hview(src, tiles[0][0], 512))
        for ti, (t0, n, w) in enumerate(tiles):
            if ti + 1 < len(tiles):
                s.dma_in('sp', hb[(ti + 1) % 2], s.hview(src, tiles[ti + 1][0], 512))
            h3, o3 = hb[ti % 2], ob[ti % 2]
            s.act(sq, h3, AF.Square)
            pst = s.ps[6]
            for ch in range(8):
                s.mm(pst, s.ones, sq[:, ch, :], ch == 0, ch == 7)
            s.act(rt, pst, AF.Sqrt, scale=1.0 / D, bias=s.epsc)
            s.recip(rstd, rt)
            for ch in range(8):
                s.stt(o3[:, ch, :], h3[:, ch, :], s.col(COL_FG + ch), rstd, ALU.mult, ALU.mult)
            s.dma_out('sp', s.hview(s.d_out, t0 - CTX, 512), o3)
        s.phase_end()


    def lbc(self, j, d, h):
        c = j * 16 + d * 8 + h
        return self.LB[:, c:c + 1], self.OML[:, c:c + 1]

    def hgrn_proj(self, i, src):
        s = self
        A = s.arena
        j = i // 2
        jl = s.cfg.get('lb_layer', j)
        Win = [A.bf16(4096) for _ in range(8)]
        for k in range(8):
            for c5, c4 in ((0, 0), (1, 1), (2, 2), (4, 3)):
                s.wload(Win[k][:, c4 * 1024:(c4 + 1) * 1024],
                        s.d_hg_in[j, k * 128:(k + 1) * 128, c5 * 1024:(c5 + 1) * 1024])
        hb = A.f32(8 * 512).re("p (c t) -> p c t", c=8)
        a3 = A.bf16(8 * 512).re("p (c t) -> p c t", c=8)
        sq = A.bf16(8 * 512).re("p (c t) -> p c t", c=8)
        rt = A.f32(512)
        rstd = A.f32(512)
        tmps = [A.f32(512) for _ in range(2)]
        qsb = [A.f32(512) for _ in range(2)]
        tb = [[A.f32(512) for _ in range(6)] for _ in range(4)]
        ob = [[[A.bf16(512) for _ in range(3)] for _ in range(4)] for _ in range(2)]
        ecl = [[A.f32(16) for _ in range(4)] for _ in range(2)]
        gst = [A.f32(512) for _ in range(2)]
        rmF = A.f32(512)
        rmB = A.f32(512)
        s.memset('pool', rmF, 1.0)
        s.memset('pool', rmB, 1.0)
        s.memset('pool', rmF.re("p (c j) -> p c j", j=32)[:, :, 0:1], 0.0)
        s.memset('pool', rmB.re("p (c j) -> p c j", j=32)[:, :, 31:32], 0.0)
        tiles = s.tiles(True)
        grp = 0
        gcn = 0
        for ti, (t0, n, w) in enumerate(tiles):
            nc32 = n // 32
            s.dma_in('sp', hb[:, :, :n], s.hview(src, t0, n))
            s.norm_mod(hb, a3, n, lambda ch: s.gmc(i, 1, ch, w), lambda ch: s.modc(i, 3, ch, w),
                       sq, rt, rstd, tmps)
            s.dma_out('sp', s.hview(s.d_AT, t0, n), a3[:, :, :n])
            for hp in range(4):
                chains = []
                for h2 in range(2):
                    h = hp * 2 + h2
                    pq = s.ps[h2 * 3]
                    for k in range(8):
                        s.mm(pq[:, :n], Win[k][:, h * 128:(h + 1) * 128], a3[:, k, :n], k == 0, k == 7)
                    q = qsb[h2]
                    s.copy('act', q[:, :n], pq[:, :n])
                    for d in range(2):
                        pz = s.ps[h2 * 3 + 1 + d]
                        c0 = 1024 + d * 1024 + h * 128
                        for k in range(8):
                            s.mm(pz[:, :n], Win[k][:, c0:c0 + 128], a3[:, k, :n], k == 0, k == 7)
                        chains.append((h, d, q, pz, len(chains)))
                bset = grp % 2
                grp += 1
                V_ = {}
                for (h, d, q, pz, ci) in chains:
                    V_[ci] = [x[:, :n] for x in tb[ci]]
                for (h, d, q, pz, ci) in chains:
                    tA = V_[ci][0]
                    s.act(tA, pz[:, :n], AF.Sigmoid)
                for (h, d, q, pz, ci) in chains:
                    tA = V_[ci][0]
                    lb, oml = s.lbc(jl, d, h)
                    s.act(tA, tA, AF.Identity, scale=oml, bias=lb)
                for (h, d, q, pz, ci) in chains:
                    tA, tB, tK = V_[ci][0], V_[ci][1], V_[ci][2]
                    s.ts('dve', tB, tA, F_MIN, None, ALU.max)
                    s.ts('pool', tK, tA, -1.0, 1.0, ALU.mult, ALU.add)
                for (h, d, q, pz, ci) in chains:
                    tB = V_[ci][1]
                    s.act(tB, tB, AF.Ln)
                for (h, d, q, pz, ci) in chains:
                    tB, tC = V_[ci][1], V_[ci][3]
                    if d == 0:
                        s.scan(tC, rmF[:, :n], tB, 0.0, ALU.mult, ALU.add)
                    else:
                        s.scan(tC[:, ::-1], rmB[:, :n][:, ::-1], tB[:, ::-1], 0.0, ALU.mult, ALU.add)
                for (h, d, q, pz, ci) in chains:
                    tC, tE = V_[ci][3], V_[ci][5]
                    c3 = tC.re("p (c j) -> p c j", j=32)
                    e = 31 if d == 0 else 0
                    s.tt('dve' if ci % 2 == 0 else 'pool', tE.re("p (c j) -> p c j", j=32),
                         c3[:, :, e:e + 1].bc([128, nc32, 32]), c3, ALU.subtract)
                for (h, d, q, pz, ci) in chains:
                    tA, tC, tD, tE = V_[ci][0], V_[ci][3], V_[ci][4], V_[ci][5]
                    s.act(tA, tC, AF.Exp)
                    s.act(tD, tC, AF.Exp, scale=-1.0)
                    s.act(tE, tE, AF.Exp)
                for (h, d, q, pz, ci) in chains:
                    tA, tK, tD, tE = V_[ci][0], V_[ci][2], V_[ci][4], V_[ci][5]
                    oQ, oK, oE = [x[:, :n] for x in ob[bset][ci]]
                    ec = ecl[bset][ci]
                    e = 31 if d == 0 else 0
                    s.tt('dve', oQ, q[:, :n], tA, ALU.mult)
                    s.tt('pool', oK, tK, tD, ALU.mult)
                    s.tt('dve', oE, tK, tE, ALU.mult)
                    s.copy('dve', ec[:, :nc32], tA.re("p (c j) -> p c j", j=32)[:, :, e])
                    s.dma_out('sp', s.d_GQ[d, h, :, t0:t0 + n], oQ)
                    s.dma_out('sp', s.d_GK[d, h, :, t0:t0 + n], oK)
                    s.dma_out('sp', s.d_GE[d, h, :, t0:t0 + n], oE)
                    s.dma_out('sp', s.d_GD[d, h, :, t0 // 32:t0 // 32 + nc32], ec[:, :nc32])
            for gc in range(8):
                pg = s.ps[gc % 4]
                for k in range(8):
                    s.mm(pg[:, :n], Win[k][:, 3072 + gc * 128:3072 + (gc + 1) * 128], a3[:, k, :n], k == 0, k == 7)
                gb = gst[gcn % 2]
                gcn += 1
                s.act(gb[:, :n], pg[:, :n], AF.Silu)
                s.dma_out('sp', s.d_GS[gc * 128:(gc + 1) * 128, t0:t0 + n], gb[:, :n])
        s.phase_end()

    def hgrn_v(self, i):
        s = self
        A = s.arena
        j = i // 2
        Wv = [A.bf16(1024) for _ in range(8)]
        for k in range(8):
            s.wload(Wv[k], s.d_hg_in[j, k * 128:(k + 1) * 128, 3072:4096])
        ab = [A.bf16(8 * 512).re("p (c t) -> p c t", c=8) for _ in range(2)]
        vt = [A.bf16(1024) for _ in range(4)]
        tiles = s.tiles(True)
        s.dma_in('sp', ab[0][:, :, :tiles[0][1]], s.hview(s.d_AT, tiles[0][0], tiles[0][1]))
        pc = 0
        vc = 0
        for ti, (t0, n, w) in enumerate(tiles):
            if ti + 1 < len(tiles):
                t1, n1, _ = tiles[ti + 1]
                s.dma_in('sp', ab[(ti + 1) % 2][:, :, :n1], s.hview(s.d_AT, t1, n1))
            a3 = ab[ti % 2]
            for sub in range(n // 128):
                vtt = vt[vc % 4]
                vc += 1
                for cg in range(2):
                    p = s.ps[pc % 6]
                    pc += 1
                    for k in range(8):
                        s.mm(p, a3[:, k, sub * 128:(sub + 1) * 128], Wv[k][:, cg * 512:(cg + 1) * 512], k == 0, k == 7)
                    s.copy('dve' if cg == 0 else 'act', vtt[:, cg * 512:(cg + 1) * 512], p)
                s.dma_out('sp', s.d_V[t0 + sub * 128:t0 + (sub + 1) * 128, :], vtt)
        s.phase_end()

    def hgrn_scan(self, d):
        s = self
        A = s.arena
        fwd = d == 0
        G = 2

        def mk(fn):
            return [[fn() for _ in range(G)] for _ in range(2)]
        Q = mk(lambda: A.bf16(4 * 512).re("p (h t) -> p h t", h=4))
        Kt = mk(lambda: A.bf16(4 * 512).re("p (h t) -> p h t", h=4))
        Ke = mk(lambda: A.bf16(4 * 512).re("p (h t) -> p h t", h=4))
        Vt = mk(lambda: A.bf16(4 * 512).re("p (j h v) -> p j h v", j=4, h=4))
        V32 = mk(lambda: A.bf16(16 * 512).re("p (c h v) -> p c h v", c=16, h=4))
        EC = mk(lambda: A.f32(4 * 16).re("p (h c) -> p h c", h=4))
        ost = mk(lambda: A.f32(4 * 512).re("p (h t) -> p h t", h=4))
        Sf = [A.f32(4 * 128).re("p (h v) -> p h v", h=4) for _ in range(G)]
        Sb = [A.bf16(4 * 128).re("p (h v) -> p h v", h=4) for _ in range(G)]
        At = [A.bf16(4 * 128).re("p (h t) -> p h t", h=4) for _ in range(G)]
        KEt = [[A.bf16(4 * 512).re("p (h c k) -> p h c k", h=4, c=4) for _ in range(G)] for _ in range(2)]
        for g in range(G):
            s.memset('pool', Sf[g], 0.0)
            s.memset('pool', Sb[g], 0.0)
        mask = s.maskF if fwd else s.maskB
        mask4 = T(mask.ap.unsqueeze(1).to_broadcast([128, 4, 128]), mask.buf)
        scP = [s.ps[0].re("p (h t) -> p h t", h=4), s.ps[1].re("p (h t) -> p h t", h=4)]
        inP = s.ps[2].re("p (h t) -> p h t", h=4)
        itP = [s.ps[3].re("p (h t) -> p h t", h=4), s.ps[4].re("p (h t) -> p h t", h=4)]
        pP = [s.ps[5].re("p (h v) -> p h v", h=4), s.ps[6].re("p (h v) -> p h v", h=4)]
        trP = T(s.ps[7].ap.bitcast(BF16), s.ps[7].buf)
        tiles = s.tiles(True)
        order = tiles if fwd else [tiles[0]] + tiles[:0:-1]
        dst = s.d_OF if fwd else s.d_OB

        def load(si):
            t0, n, w = order[si]
            b = si % 2
            for g in range(G):
                q_ = 'sp' if g == 0 else 'act'
                hs = slice(4 * g, 4 * g + 4)
                cs_ = slice(512 * g, 512 * g + 512)
                s.dma_in(q_, Q[b][g][:, :, :n], s.d_GQ[d, hs, :, t0:t0 + n].rearrange("h p t -> p h t"))
                s.dma_in(q_, Kt[b][g][:, :, :n], s.d_GK[d, hs, :, t0:t0 + n].rearrange("h p t -> p h t"))
                s.dma_in(q_, Ke[b][g][:, :, :n], s.d_GE[d, hs, :, t0:t0 + n].rearrange("h p t -> p h t"))
                s.dma_in(q_, Vt[b][g][:, :n // 128].re("p j h v -> p j (h v)"),
                         s.d_V[t0:t0 + n, cs_].rearrange("(j p) c -> p j c", p=128))
                s.dma_in(q_, V32[b][g][0:32, :n // 32].re("p c h v -> p c (h v)"),
                         s.d_V[t0:t0 + n, cs_].rearrange("(c p) x -> p c x", p=32))
                s.dma_in(q_, EC[b][g][:, :, :n // 32],
                         s.d_GD[d, hs, :, t0 // 32:(t0 + n) // 32].rearrange("h p c -> p h c"))

        load(0)
        kc = 0
        for si, (t0, n, w) in enumerate(order):
            if si + 1 < len(order):
                load(si + 1)
            b = si % 2
            njt = n // 128
            jts = list(range(njt)) if fwd else list(range(njt - 1, -1, -1))
            cs_ = [0, 1, 2, 3] if fwd else [3, 2, 1, 0]
            for jt in jts:
                ts0 = jt * 128
                kets = []
                for g in range(G):
                    for h in range(4):
                        s.mm(scP[g][:, h, :], Kt[b][g][:, h, ts0:ts0 + 128], Q[b][g][:, h, ts0:ts0 + 128])
                    s.tt('dve', At[g], scP[g], mask4, ALU.mult)
                    ke = KEt[kc % 2][g]
                    for hp in range(2):
                        for h2 in range(2):
                            h = hp * 2 + h2
                            for c in range(4):
                                o0 = (h2 * 4 + c) * 128
                                s.tr(trP[0:32, o0:o0 + 128], Ke[b][g][:, h, ts0 + c * 32:ts0 + (c + 1) * 32], s.ident)
                        s.copy('act', ke[0:32, 2 * hp:2 * hp + 2].re("p h c k -> p (h c k)"), trP[0:32, :])
                    kets.append(ke)
                    for h in range(4):
                        s.mm(inP[:, h, :], Vt[b][g][:, jt, h, :], At[g][:, h, :])
                    s.copy('act', ost[b][g][:, :, ts0:ts0 + 128], inP)
                kc += 1
                for c in cs_:
                    ch = jt * 4 + c
                    for g in range(G):
                        for h in range(4):
                            s.mm(itP[g][:, h, c * 32:(c + 1) * 32], Sb[g][:, h, :],
                                 Q[b][g][:, h, ts0 + c * 32:ts0 + (c + 1) * 32])
                        for h in range(4):
                            s.mm(pP[g][:, h, :], kets[g][0:32, h, c, :], V32[b][g][0:32, ch, h, :])
                        s.tt('dve' if g == 0 else 'pool', Sf[g], Sf[g],
                             EC[b][g][:, :, ch:ch + 1].bc([128, 4, 128]), ALU.mult)
                        s.tt('dve', Sf[g], Sf[g], pP[g], ALU.add)
                        s.copy('act', Sb[g], Sf[g])
                for g in range(G):
                    o_ = ost[b][g][:, :, ts0:ts0 + 128]
                    s.tt('dve', o_, o_, itP[g], ALU.add)
            for g in range(G):
                s.dma_out('sp', dst.rearrange("(h p) t -> p h t", p=128)[:, 4 * g:4 * g + 4, t0:t0 + n],
                          ost[b][g][:, :, :n])
        s.phase_end()

    def hgrn_post(self, i, with_ctx):
        s = self
        A = s.arena
        j = i // 2
        of = [A.f32(8 * 512).re("p (c t) -> p c t", c=8) for _ in range(2)]
        ob_ = [A.f32(8 * 512).re("p (c t) -> p c t", c=8) for _ in range(2)]
        gs = [A.f32(8 * 512).re("p (c t) -> p c t", c=8) for _ in range(2)]
        sq = A.bf16(8 * 512).re("p (c t) -> p c t", c=8)
        rt = A.f32(8 * 512).re("p (c t) -> p c t", c=8)
        r3 = [A.bf16(8 * 512).re("p (c t) -> p c t", c=8) for _ in range(2)]
        tiles = s.tiles(with_ctx)
        psall = T(s.psum_all.rearrange("p (c t) -> p c t", c=8), tuple(p.buf for p in s.ps))

        def load(ti):
            t0, n, w = tiles[ti]
            s.dma_in('sp', of[ti % 2][:, :, :n], s.hview(s.d_OF, t0, n))
            s.dma_in('act', ob_[ti % 2][:, :, :n], s.hview(s.d_OB, t0, n))
            s.dma_in('sp', gs[ti % 2][:, :, :n], s.hview(s.d_GS, t0, n))

        load(0)
        for ti, (t0, n, w) in enumerate(tiles):
            if ti + 1 < len(tiles):
                load(ti + 1)
            b = ti % 2
            o3 = of[b]
            s.tt('pool', o3[:, :, :n], o3[:, :, :n], ob_[b][:, :, :n], ALU.add)
            s.act(sq[:, :, :n], o3[:, :, :n], AF.Square)
            for h in range(8):
                s.mm(s.ps[h][:, :n], s.ones, sq[:, h, :n])
            s.act(rt[:, :, :n], psall[:, :, :n], AF.Sqrt, scale=1.0 / 128, bias=s.epsc)
            s.recip(rt[:, :, :n], rt[:, :, :n])
            s.tt('dve', o3[:, :, :n], o3[:, :, :n], rt[:, :, :n], ALU.mult)
            s.stt(r3[b][:, :, :n], o3[:, :, :n], s.col(COL_GN + j), gs[b][:, :, :n], ALU.mult, ALU.mult)
            s.dma_out('sp', s.hview(s.d_OT, t0, n), r3[b][:, :, :n])
        s.phase_end()

    def hgrn_layer(self, i, with_ctx, pidx=0, nxt=None):
        st = self.cfg.get('hg_stop', 5)
        self.hgrn_proj(i, self.d_H)
        self.hgrn_v(i)
        if st >= 2:
            self.hgrn_scan(0)
        if st >= 3:
            self.hgrn_scan(1)
        if st >= 4:
            self.hgrn_post(i, with_ctx)
        if st >= 5:
            self.outproj(self.d_hg_out[i // 2], i, with_ctx, pidx, nxt)

    def rot_cols(self, dst, src):
        s = self
        d4 = dst.re("p g (a h i) -> p g a h i", a=2, h=2)
        s4 = src.re("p g (a h i) -> p g a h i", a=2, h=2)
        for a in range(2):
            s.ts('pool', d4[:, :, a, 0, :], s4[:, :, a, 1, :], -1.0, None, ALU.mult)
            s.copy('pool', d4[:, :, a, 1, :], s4[:, :, a, 0, :])

    def mla_proj(self, i, src):
        s = self
        A = s.arena
        j = i // 2
        Wdn = [A.bf16(704) for _ in range(8)]
        Wdr = [A.bf16(64) for _ in range(8)]
        Wuq = [A.bf16(1536) for _ in range(3)]
        Wqr = [A.bf16(512) for _ in range(3)]
        Wkv = [A.bf16(2048) for _ in range(2)]
        for k in range(8):
            s.wload(Wdn[k], s.d_mla_down[j, k * 128:(k + 1) * 128, :])
        for k in range(3 if 'wq' not in s.cfg.get('mp_skip', ()) else 0):
            s.wload(Wuq[k], s.d_mla_uq[j, k * 128:(k + 1) * 128, :])
        for k in range(2 if 'wkv' not in s.cfg.get('mp_skip', ()) else 0):
            s.wload(Wkv[k], s.d_mla_ukv[j, k * 128:(k + 1) * 128, :])
        for k in range(8 if 'rot' not in s.cfg.get('mp_skip', ()) else 0):
            s.rot_cols(Wdr[k].re("p (g c) -> p g c", g=1), Wdn[k][:, 640:704].re("p (g c) -> p g c", g=1))
        for k in range(3 if 'rot' not in s.cfg.get('mp_skip', ()) else 0):
            s.rot_cols(Wqr[k].re("p (g c) -> p g c", g=8), Wuq[k].re("p (g c) -> p g c", g=8)[:, :, 128:192])
        hb = [A.f32(8 * 512).re("p (c t) -> p c t", c=8) for _ in range(2)]
        a3 = A.bf16(8 * 512).re("p (c t) -> p c t", c=8)
        sq = A.bf16(8 * 512).re("p (c t) -> p c t", c=8)
        rt = A.f32(512)
        rstd = A.f32(512)
        tmps = [A.f32(512) for _ in range(2)]
        cqf = [A.f32(512) for _ in range(5)]
        sqd = A.bf16(5 * 512).re("p (c t) -> p c t", c=5)
        cqn = [A.bf16(512) for _ in range(5)]
        rt2 = [A.f32(512) for _ in range(2)]
        rs2 = [A.f32(512) for _ in range(2)]
        cs = [A.f32(2 * 512).re("p (c t) -> p c t", c=2) for _ in range(2)]
        r1 = [A.f32(512) for _ in range(2)]
        r2 = [A.f32(512) for _ in range(2)]
        krb = [A.bf16(512) for _ in range(2)]
        qn_all = [A.bf16(8 * 512).re("p (h t) -> p h t", h=8) for _ in range(2)]
        qr_all = [A.bf16(8 * 512).re("p (h t) -> p h t", h=8) for _ in range(2)]
        kn_all = [A.bf16(8 * 512).re("p (h t) -> p h t", h=8) for _ in range(2)]
        vt = [A.bf16(1024) for _ in range(2)]
        tiles = s.tiles(True)
        rope3 = s.d_rope.rearrange("c p t -> p c t")

        def load(ti):
            t0, n, w = tiles[ti]
            s.dma_in('sp', hb[ti % 2][:, :, :n], s.hview(src, t0, n))
            if 'rope' not in s.cfg.get('mp_skip', ()):
                s.dma_in('sp', cs[ti % 2][0:64, :, :n], rope3[:, :, t0:t0 + n])

        load(0)
        pc = 0
        for ti, (t0, n, w) in enumerate(tiles):
            if ti + 1 < len(tiles):
                load(ti + 1)
            b = ti % 2
            h3 = hb[b]
            cosT, sinT = cs[b][0:64, 0, :n], cs[b][0:64, 1, :n]
            s.norm_mod(h3, a3, n, lambda ch: s.gmc(i, 1, ch, w), lambda ch: s.modc(i, 3, ch, w),
                       sq, rt, rstd, tmps)
            for oc in range(5):
                p = s.ps[pc % 4]
                pc += 1
                for k in range(8):
                    s.mm(p[:, :n], Wdn[k][:, oc * 128:(oc + 1) * 128], a3[:, k, :n], k == 0, k == 7)
                s.copy('dve', cqf[oc][:, :n], p[:, :n])
                s.act(sqd[:, oc, :n], cqf[oc][:, :n], AF.Square)
            SK_ = s.cfg.get('mp_skip', ())
            pk, pr = s.ps[4], s.ps[5]
            if 'kr' in SK_:
                continue
            for k in range(8):
                s.mm(pk[0:64, :n], Wdn[k][:, 640:704], a3[:, k, :n], k == 0, k == 7)
            for k in range(8):
                s.mm(pr[0:64, :n], Wdr[k], a3[:, k, :n], k == 0, k == 7)
            s.tt('dve', r1[0][0:64, :n], pk[0:64, :n], cosT, ALU.mult)
            s.tt('dve', r2[0][0:64, :n], pr[0:64, :n], sinT, ALU.mult)
            s.tt('pool', krb[b][0:64, :n], r1[0][0:64, :n], r2[0][0:64, :n], ALU.add)
            s.dma_out('sp', s.d_KR[:, t0:t0 + n], krb[b][0:64, :n])
            if 'norm' in SK_:
                continue
            for g, (c0, c1, dd, colb) in enumerate(((0, 3, 384, COL_QN + j * 3), (3, 5, 256, COL_KVN + j * 2))):
                pst = s.ps[6 + g]
                for c in range(c0, c1):
                    s.mm(pst[:, :n], s.ones, sqd[:, c, :n], c == c0, c == c1 - 1)
                s.act(rt2[g][:, :n], pst[:, :n], AF.Sqrt, scale=1.0 / dd, bias=s.epsc)
                s.recip(rs2[g][:, :n], rt2[g][:, :n])
                for c in range(c0, c1):
                    s.stt(cqn[c][:, :n], cqf[c][:, :n], s.col(colb + c - c0), rs2[g][:, :n], ALU.mult, ALU.mult)
            if 'q' in SK_:
                continue
            for h in range(8):
                p = s.ps[pc % 4]
                pc += 1
                for c in range(3):
                    s.mm(p[:, :n], Wuq[c][:, h * 192:h * 192 + 128], cqn[c][:, :n], c == 0, c == 2)
                s.copy('act', qn_all[b][:, h, :n], p[:, :n])
                pk, pr = s.ps[4], s.ps[5]
                for c in range(3):
                    s.mm(pk[0:64, :n], Wuq[c][:, h * 192 + 128:h * 192 + 192], cqn[c][:, :n], c == 0, c == 2)
                for c in range(3):
                    s.mm(pr[0:64, :n], Wqr[c][:, h * 64:(h + 1) * 64], cqn[c][:, :n], c == 0, c == 2)
                s.tt('dve', r1[h % 2][0:64, :n], pk[0:64, :n], cosT, ALU.mult)
                s.tt('dve', r2[h % 2][0:64, :n], pr[0:64, :n], sinT, ALU.mult)
                s.tt('pool', qr_all[b][0:64, h, :n], r1[h % 2][0:64, :n], r2[h % 2][0:64, :n], ALU.add)
            s.dma_out('sp', s.d_QN.rearrange("h p t -> p h t")[:, :, t0:t0 + n], qn_all[b][:, :, :n])
            s.dma_out('sp', s.d_QR.rearrange("h p t -> p h t")[:, :, t0:t0 + n], qr_all[b][0:64, :, :n])
            if 'kn' in SK_:
                continue
            for h in range(8):
                p = s.ps[pc % 4]
                pc += 1
                for c in range(2):
                    s.mm(p[:, :n], Wkv[c][:, h * 256:h * 256 + 128], cqn[3 + c][:, :n], c == 0, c == 1)
                s.copy('act' if h % 2 == 0 else 'dve', kn_all[b][:, h, :n], p[:, :n])
            s.dma_out('sp', s.d_KN.rearrange("h p t -> p h t")[:, :, t0:t0 + n], kn_all[b][:, :, :n])
            if 'v' in SK_:
                continue
            for sub in range(n // 128):
                vtt = vt[sub % 2]
                for cg in range(2):
                    p = s.ps[pc % 4]
                    pc += 1
                    for c in range(2):
                        rhs = Wkv[c].re("p (h x) -> p h x", h=8)[:, 4 * cg:4 * cg + 4, 128:256]
                        s.mm(p.re("p (h x) -> p h x", h=4), cqn[3 + c][:, sub * 128:(sub + 1) * 128], rhs,
                             c == 0, c == 1)
                    s.copy('act' if cg == 0 else 'dve', vtt[:, cg * 512:(cg + 1) * 512], p)
                s.dma_out('sp', s.d_V[t0 + sub * 128:t0 + (sub + 1) * 128, :], vtt)
        s.phase_end()

    def mla_attn(self, i, with_ctx):
        s = self
        A = s.arena
        T_ = s.T
        NKT = T_ // 128
        KR = A.bf16(T_)
        s.memset('pool', KR[64:128, :], 0.0)
        s.dma_in('sp', KR[0:64, :], s.d_KR)
        KN = [A.bf16(T_) for _ in range(2)]
        V = [A.bf16(T_).re("p (k v) -> p k v", v=128) for _ in range(2)]
        QN = [A.bf16(512) for _ in range(2)]
        QR = [A.bf16(512) for _ in range(2)]
        for qq in QR:
            s.memset('pool', qq[64:128, :], 0.0)
        P = [A.bf16(512) for _ in range(6)]
        accD = [A.f32(512) for _ in range(2)]
        accP = [A.f32(512) for _ in range(2)]
        rl = [A.f32(512) for _ in range(2)]
        oT = [A.bf16(512) for _ in range(2)]
        tiles = s.tiles(with_ctx)
        jobs = [(h, ti) for h in range(8) for ti in range(len(tiles))]
        bg = []
        if s.bg_host is not None and s.bg_host[0] == i:
            bg = s.mod_bg_steps(s.bg_host[1], A)

        def loadkv(h):
            s.dma_in('sp', KN[h % 2], s.d_KN[h])
            s.dma_in('act', V[h % 2], s.d_V.rearrange("(k p) c -> p k c", p=128)[:, :, h * 128:(h + 1) * 128])

        def loadq(jn):
            h, ti = jobs[jn]
            t0, n, w = tiles[ti]
            s.dma_in('sp', QN[jn % 2][:, :n], s.d_QN[h, :, t0:t0 + n])
            s.dma_in('sp', QR[jn % 2][0:64, :n], s.d_QR[h, :, t0:t0 + n])

        loadkv(0)
        loadq(0)
        sc = 0
        for jn, (h, ti) in enumerate(jobs):
            if ti == 0 and h + 1 < 8:
                loadkv(h + 1)
            if jn + 1 < len(jobs):
                loadq(jn + 1)
            t0, n, w = tiles[ti]
            kts = [0, 1] if w else list(range(NKT))
            qn, qr = QN[jn % 2], QR[jn % 2]
            kn, v = KN[h % 2], V[h % 2]
            O, L = s.ps[3 + jn % 2], s.ps[5]
            if jn < len(bg):
                bg[jn]()
            Sb = {}

            def qk(x):
                kt = kts[x]
                S_ = s.ps[(sc + x) % 3]
                s.mm(S_[:, :n], kn[:, kt * 128:(kt + 1) * 128], qn[:, :n], True, False)
                s.mm(S_[:, :n], KR[:, kt * 128:(kt + 1) * 128], qr[:, :n], False, True)
                Pt = P[(sc + x) % 6]
                s.act(Pt[:, :n], S_[:, :n], AF.Exp, scale=MLA_SCALE)
                Sb[x] = Pt

            def pv(x):
                kt = kts[x]
                Pt = Sb.pop(x)
                s.mm(O[:, :n], v[:, kt, :], Pt[:, :n], x == 0, x == len(kts) - 1)
                if x % 4 == 3:
                    eng, acc, first = 'pool', accP[jn % 2], x == 3
                else:
                    eng, acc, first = 'dve', accD[jn % 2], x == 0
                if first:
                    s.copy(eng, acc[:, :n], Pt[:, :n])
                else:
                    s.tt(eng, acc[:, :n], acc[:, :n], Pt[:, :n], ALU.add)

            SK = 2
            for x in range(len(kts) + SK):
                if x < len(kts):
                    qk(x)
                if x - SK >= 0:
                    pv(x - SK)
            sc += len(kts)
            twoacc = len(kts) > 3
            s.mm(L[:, :n], s.onesf, accD[jn % 2][:, :n], True, not twoacc)
            if twoacc:
                s.mm(L[:, :n], s.onesf, accP[jn % 2][:, :n], False, True)
            s.recip(rl[jn % 2][:, :n], L[:, :n])
            s.tt('dve', oT[jn % 2][:, :n], O[:, :n], rl[jn % 2][:, :n], ALU.mult)
            s.dma_out('sp', s.d_OT[h * 128:(h + 1) * 128, t0:t0 + n], oT[jn % 2][:, :n])
        for st in bg[len(jobs):]:
            st()
        s.phase_end()

    def outproj(self, wdram, i, with_ctx, pidx=0, nxt=None):
        s = self
        A = s.arena
        s.ffn_wsets()
        Wo = [A.bf16(1024) for _ in range(8)]
        for k in range(8):
            s.wload(Wo[k], wdram[k * 128:(k + 1) * 128, :])
        if nxt is not None:
            s.ffn_wload(nxt, pidx % 2)
        hb = [A.f32(8 * 512).re("p (c t) -> p c t", c=8) for _ in range(2)]
        ob = [A.bf16(8 * 512).re("p (c t) -> p c t", c=8) for _ in range(2)]
        tiles = s.tiles(with_ctx)

        def load(ti):
            t0, n, w = tiles[ti]
            s.dma_in('sp', hb[ti % 2][:, :, :n], s.hview(s.d_H, t0, n))
            s.dma_in('sp', ob[ti % 2][:, :, :n], s.hview(s.d_OT, t0, n))

        load(0)
        for ti, (t0, n, w) in enumerate(tiles):
            if ti + 1 < len(tiles):
                load(ti + 1)
            h3, o3 = hb[ti % 2], ob[ti % 2]
            for dch in range(8):
                py = s.ps[dch % 4]
                for k in range(8):
                    s.mm(py[:, :n], Wo[k][:, dch * 128:(dch + 1) * 128], o3[:, k, :n], k == 0, k == 7)
                s.stt(h3[:, dch, :n], py[:, :n], s.hgc(i, 1, dch, w), h3[:, dch, :n], ALU.mult, ALU.add)
            s.dma_out('sp', s.hview(s.d_H, t0, n), h3[:, :, :n])
        s.phase_end()

    def mla_layer(self, i, with_ctx, pidx=0, nxt=None):
        st = self.cfg.get('mla_stop', 3)
        self.mla_proj(i, self.d_H)
        if st >= 2:
            self.mla_attn(i, with_ctx)
        if st >= 3:
            self.outproj(self.d_mla_o[i // 2], i, with_ctx, pidx, nxt)

    def build(self):
        s = self
        nc = s.nc
        T_ = s.T
        dp = s.depth
        s.d_xT = s.din("xT", [D, T_])
        s.d_cols = s.din("cols", [128, NCOLS])
        s.d_consts = s.din("consts", [128, 384])
        s.d_mod_w = s.din("mod_w", [4, D, NMOD * D])
        s.d_wg = s.din("ffn_w_gate", [4, 2, D, 2816])
        s.d_wu = s.din("ffn_w_up", [4, 2, D, 2816])
        s.d_wd = s.din("ffn_w_down", [4, 2, 2816, D])
        s.d_out = nc.dram_tensor("outT", [D, s.TL], F32, kind="ExternalOutput").ap()
        s.d_rope = s.din("rope", [2, 64, T_])
        s.d_mla_down = s.din("mla_w_down", [2, D, 704])
        s.d_mla_uq = s.din("mla_w_uq", [2, 384, 1536])
        s.d_mla_ukv = s.din("mla_w_ukv", [2, 256, 2048])
        s.d_mla_o = s.din("mla_w_o", [2, D, D])
        s.d_KR = s.dscr("KR", [64, T_], BF16)
        s.d_QN = s.dscr("QN", [8, 128, T_], BF16)
        s.d_QR = s.dscr("QR", [8, 64, T_], BF16)
        s.d_KN = s.dscr("KN", [8, 128, T_], BF16)
        s.d_V = s.dscr("V", [T_, D], BF16)
        s.d_OT = s.dscr("OT", [D, T_], BF16)
        s.d_hg_in = s.din("hg_w_in", [2, D, 5120])
        s.d_hg_out = s.din("hg_w_out", [2, D, D])
        s.d_GQ = s.dscr("GQ", [2, 8, 128, T_], BF16)
        s.d_GK = s.dscr("GK", [2, 8, 128, T_], BF16)
        s.d_GE = s.dscr("GE", [2, 8, 128, T_], BF16)
        s.d_GD = s.dscr("GD", [2, 8, 128, T_ // 32], F32)
        s.d_GS = s.dscr("GS", [D, T_], F32)
        s.d_OF = s.dscr("OF", [D, T_], F32)
        s.d_OB = s.dscr("OB", [D, T_], F32)
        s.d_H = s.dscr("H", [D, T_], F32)
        s.d_AT = s.dscr("AT", [D, T_], BF16)
        with ExitStack() as es:
            arena = es.enter_context(nc.sbuf_tensor("sb_arena", [128, 49 * 1024], F32))
            s.arena = Arena(arena[:, :])
            s.cols = T(es.enter_context(nc.sbuf_tensor("sb_cols", [128, NCOLS], F32))[:, :])
            s.MODC = T(es.enter_context(nc.sbuf_tensor("sb_modc", [128, 4 * NMOD * NCH * 2], F32))[:, :])
            s.GM = T(es.enter_context(nc.sbuf_tensor("sb_gm", [128, 4 * 3 * NCH * 2], F32))[:, :])
            s.HG = T(es.enter_context(nc.sbuf_tensor("sb_hg", [128, 4 * 3 * NCH * 2], F32))[:, :])
            s.LB = T(es.enter_context(nc.sbuf_tensor("sb_lb", [128, 32], F32))[:, :])
            s.OML = T(es.enter_context(nc.sbuf_tensor("sb_oml", [128, 32], F32))[:, :])
            s.cst = T(es.enter_context(nc.sbuf_tensor("sb_cst", [128, 384], BF16))[:, :])
            s.ones = T(es.enter_context(nc.sbuf_tensor("sb_ones", [128, 128], BF16))[:, :])
            epst = es.enter_context(nc.sbuf_tensor("sb_epsc", [128, 1], F32))
            s.epsT = T(epst[:, :])
            s.ident = s.cst[:, 0:128]
            s.maskF = s.cst[:, 128:256]
            s.maskB = s.cst[:, 256:384]
            psum = es.enter_context(nc.psum_tensor("ps_psum", [128, 4096], F32))
            s.ps = [T(psum[:, b * 512:(b + 1) * 512]) for b in range(8)]
            s.psum_all = psum[:, :]
            s.memset('pool', s.epsT, RMS_EPS)
            s.epsc = s.epsT
            s.identf = T(es.enter_context(nc.sbuf_tensor("sb_identf", [128, 128], F32))[:, :])
            s.scbf = T(es.enter_context(nc.sbuf_tensor("sb_scbf", [128, 16], BF16))[:, :]).re("p (c w) -> p c w", w=2)
            s.onesf = T(es.enter_context(nc.sbuf_tensor("sb_onesf", [128, 128], F32))[:, :])
            s.memset('pool', s.onesf, 1.0)
            s.onec = T(es.enter_context(nc.sbuf_tensor("sb_onec", [128, 1], F32))[:, :])
            s.memset('pool', s.onec, 1.0)

            s.program()

            s.S.finalize()
            csem = {e: [es.enter_context(nc.semaphore("c_%s_%d" % (e, k))) for k in range(s.S.nsem[e])]
                    for e in ('pe', 'act', 'dve', 'pool')}
            dsem = {q: [es.enter_context(nc.semaphore("d_%s_%d" % (q, k))) for k in range(NSLOT)] for q in DMAQ}
            block = es.enter_context(nc.Block())

            @block.tensor
            def _(e):
                s.S.emit_engine('pe', e, csem, dsem)

            @block.scalar
            def _(e):
                s.S.emit_engine('act', e, csem, dsem)

            @block.vector
            def _(e):
                s.S.emit_engine('dve', e, csem, dsem)

            @block.gpsimd
            def _(e):
                s.S.emit_engine('pool', e, csem, dsem)

            @block.sync
            def _(e):
                s.S.emit_engine('sp', e, csem, dsem)
        return nc

    def program(self):
        s = self
        s.phase_consts()
        mixes0 = s.cfg.get('mix', ['hgrn', 'mla', 'hgrn', 'mla'])
        s.bg_host = None
        if s.cfg.get('mod_bg', True) and s.depth == 4 and mixes0[1] == 'mla':
            s.phase_mod([0, 1])
            s.bg_host = (1, [2, 3])
        else:
            s.phase_mod(list(range(s.depth)))
        ph = []
        src = s.d_xT
        mixes = s.cfg.get('mix', ['hgrn', 'mla', 'hgrn', 'mla'])
        for i in range(s.depth):
            ph += s.ffn_descs(i, 0, src, s.d_H)
            src = s.d_H
            last = (i == s.depth - 1) and not s.cfg.get('always_ctx', False)
            if mixes[i] is not None:
                ph.append((mixes[i], (i, not last)))
            ph += s.ffn_descs(i, 1, s.d_H, s.d_H, with_ctx=not last)
        pidx = 0
        prefetched = False
        for x, (kind, d) in enumerate(ph):
            nxt = ph[x + 1][1] if (x + 1 < len(ph) and ph[x + 1][0] == 'ffn') else None
            if not s.cfg.get('prefetch', True):
                nxt = None
            if kind == 'ffn':
                s.ffn_pass(d, pidx, prefetched, nxt)
                pidx += 1
            elif kind == 'mla':
                s.mla_layer(d[0], d[1], pidx, nxt)
            elif kind == 'hgrn':
                s.hgrn_layer(d[0], d[1], pidx, nxt)
            prefetched = nxt is not None
        s.phase_final(s.d_H)


def colify(v):
    v = np.asarray(v, np.float32)
    return np.ascontiguousarray(v.reshape(-1, 128).T)


def host_consts():
    ident = np.eye(128, dtype=np.float32)
    s_ = np.arange(128)[:, None]
    t_ = np.arange(128)[None, :]
    same = (s_ // 32) == (t_ // 32)
    maskF = (same & (s_ <= t_)).astype(np.float32)
    maskB = (same & (s_ >= t_)).astype(np.float32)
    return np.concatenate([ident, maskF, maskB], axis=1)


def rope_tables(TL):
    rows = TL // 64
    row = np.repeat(np.arange(rows), 64).astype(np.float32)
    colp = np.tile(np.arange(64), rows).astype(np.float32)
    inv = (1.0 / (np.float32(10000.0) ** (np.arange(0, 32, 2, dtype=np.float32) / np.float32(32)))).astype(np.float32)
    out = np.zeros((2, 64, CTX + TL), np.float32)
    out[0, :, :CTX] = 1.0
    for ax, pos in enumerate((row, colp)):
        ang = (pos[None, :] * inv[:, None]).astype(np.float32)
        for hf in range(2):
            r0 = ax * 32 + hf * 16
            out[0, r0:r0 + 16, CTX:] = np.cos(ang)
            out[1, r0:r0 + 16, CTX:] = np.sin(ang)
    return out


def pack_cols(inp, b):
    cols = np.zeros((128, NCOLS), np.float32)
    cols[:, COL_C:COL_C + 8] = colify(inp['c'][b])
    cols[:, COL_C + 8:COL_C + 16] = colify(inp['c_ctx'])
    nl = inp['mod_b'].shape[0]
    for i in range(nl):
        cols[:, COL_MODB + i * 72:COL_MODB + (i + 1) * 72] = colify(inp['mod_b'][i])
        for k in range(3):
            cols[:, COL_NG + (i * 3 + k) * 8:COL_NG + (i * 3 + k + 1) * 8] = colify(inp['norm_g'][i, k])
    cols[:, COL_FG:COL_FG + 8] = colify(inp['final_g'])
    for j in range(inp['hg_gn'].shape[0]):
        cols[:, COL_GN + j] = inp['hg_gn'][j]
        for d in range(2):
            cols[:, COL_LB + (j * 2 + d) * 8:COL_LB + (j * 2 + d + 1) * 8] = colify(inp['hg_lb_logits'][j, d])
        cols[:, COL_QN + j * 3:COL_QN + j * 3 + 3] = colify(inp['mla_q_norm'][j])
        cols[:, COL_KVN + j * 2:COL_KVN + j * 2 + 2] = colify(inp['mla_kv_norm'][j])
    return cols


_CACHE = {}


def run(inp, cfg, n_cores):
    inp = {k: np.asarray(v) for k, v in inp.items()}
    key = repr(sorted(cfg.items()))
    kb = K(cfg)
    nc = kb.build()
    consts = host_consts()
    rope = rope_tables(kb.TL)
    in_maps = []
    for b in range(n_cores):
        xT = np.ascontiguousarray(np.concatenate([inp['ctx'][b], inp['x'][b]], axis=0).T.astype(np.float32))
        m = {"xT": xT, "cols": pack_cols(inp, b), "consts": consts,
             "mod_w": inp['mod_w'], "ffn_w_gate": inp['ffn_w_gate'], "ffn_w_up": inp['ffn_w_up'],
             "ffn_w_down": inp['ffn_w_down'], "rope": rope,
             "mla_w_down": inp['mla_w_down'], "mla_w_uq": inp['mla_w_uq'], "mla_w_ukv": inp['mla_w_ukv'],
             "mla_w_o": inp['mla_w_o'], "hg_w_in": inp['hg_w_in'], "hg_w_out": inp['hg_w_out']}
        in_maps.append(m)
    res = run_bass_kernel_spmd(nc, in_maps, core_ids=list(range(n_cores)))
    return res.results


def kernel(**inputs):
    res = run(inputs, {}, 8)
    out = np.stack([np.ascontiguousarray(r["outT"].T) for r in res], axis=0)
    return out.astype(np.float32)
```

```python
import numpy as np
from contextlib import ExitStack
import concourse.bass as bass
import concourse.mybir as mybir
from concourse.bass_utils import run_bass_kernel_spmd

F32 = mybir.dt.float32
BF16 = mybir.dt.bfloat16
AF = mybir.ActivationFunctionType
ALU = mybir.AluOpType

D = 1024
NCH = 8
CTX = 256
NFC = 22
NMOD = 9
RMS_EPS = 1e-6
F_MIN = 1e-6
MLA_SCALE = 192 ** -0.5

ENGS = ('pe', 'act', 'dve', 'pool', 'sp')
DMAQ = ('sp', 'act', 'pool')
NSLOT = 10
SEMCAP = 30000


class Buf:
    __slots__ = ('w', 'r', 'rd')

    def __init__(self):
        self.w = None
        self.r = {}
        self.rd = []


class Op:
    __slots__ = ('eng', 'fn', 'deps', 'dma', 'sig', 'semi', 'val', 'prev', 'slot', 'idx')


class Sched:
    def __init__(self):
        self.ops = {e: [] for e in ENGS}
        self.bar = {e: [] for e in ENGS}
        self.dma_since = []

    def add(self, eng, fn, reads=(), writes=(), dma=False):
        op = Op()
        op.eng, op.fn, op.dma, op.sig = eng, fn, dma, False
        op.idx = len(self.ops[eng])
        deps = {}

        def flat(bs):
            out = []
            for b in bs:
                if isinstance(b, (tuple, list)):
                    out.extend(b)
                else:
                    out.append(b)
            return out
        reads = flat(reads)
        writes = flat(writes)

        def need(d, raw):
            if d is None or d is op:
                return
            if d.dma:
                deps[id(d)] = d
                return
            if (not dma) and d.eng == eng:
                if eng == 'pe':
                    return
            k = d.eng
            if k not in deps or deps[k].idx < d.idx:
                deps[k] = d

        for b in reads:
            need(b.w, True)
        for b in writes:
            need(b.w, False)
            for r in b.r.values():
                need(r, False)
            for r in b.rd:
                need(r, False)
        for d in self.bar[eng]:
            need(d, True)
        self.bar[eng] = []
        for b in reads:
            if dma:
                b.rd.append(op)
            else:
                b.r[eng] = op
        for b in writes:
            b.w = op
            b.r = {}
            b.rd = []
        op.deps = list(deps.values())
        for d in op.deps:
            d.sig = True
        self.ops[eng].append(op)
        if dma:
            self.dma_since.append(op)
        return op

    def barrier(self):
        tails = [self.ops[e][-1] for e in ENGS if self.ops[e] and not self.ops[e][-1].dma]
        tails = []
        for e in ENGS:
            for op in reversed(self.ops[e]):
                if not op.dma:
                    tails.append(op)
                    break
        tails += self.dma_since
        self.dma_since = []
        for e in ENGS:
            self.bar[e] = self.bar[e] + tails

    def finalize(self):
        self.nsem = {}
        for e in ('pe', 'act', 'dve', 'pool'):
            cnt = 0
            for op in self.ops[e]:
                if op.dma:
                    continue
                if op.sig:
                    op.semi = cnt // SEMCAP
                    op.val = cnt % SEMCAP + 1
                    cnt += 1
            self.nsem[e] = cnt // SEMCAP + 1
        self.dma_final = {}
        for q in DMAQ:
            cnt = [0] * NSLOT
            j = 0
            for op in self.ops[q]:
                if not op.dma:
                    continue
                sl = j % NSLOT
                op.slot = sl
                op.prev = cnt[sl] * 16
                cnt[sl] += 1
                op.val = cnt[sl] * 16
                j += 1
            self.dma_final[q] = [c * 16 for c in cnt]

    def emit_engine(self, e, eng, csem, dsem):
        known = {}

        def wait(key, sem, val):
            if known.get(key, 0) < val:
                eng.wait_ge(sem, val)
                known[key] = val

        for op in self.ops[e]:
            for d in op.deps:
                if d.dma:
                    wait(('d', d.eng, d.slot), dsem[d.eng][d.slot], d.val)
                else:
                    wait(('c', d.eng, d.semi), csem[d.eng][d.semi], d.val)
            if op.dma and op.prev > 0:
                wait(('d', e, op.slot), dsem[e][op.slot], op.prev)
            ins = op.fn(eng)
            if op.dma:
                ins.then_inc(dsem[e][op.slot], 16)
            elif op.sig:
                ins.then_inc(csem[e][op.semi], 1)
        if e == 'sp':
            for q in DMAQ:
                for sl in range(NSLOT):
                    if self.dma_final[q][sl] > 0:
                        wait(('d', q, sl), dsem[q][sl], self.dma_final[q][sl])


class T:
    __slots__ = ('ap', 'buf')

    def __init__(self, ap, buf=None):
        self.ap = ap
        self.buf = buf if buf is not None else Buf()

    def __getitem__(self, k):
        return T(self.ap[k], self.buf)

    def re(self, pat, **kw):
        return T(self.ap.rearrange(pat, **kw), self.buf)

    def bc(self, shape):
        return T(self.ap.to_broadcast(shape), self.buf)


class Arena:
    def __init__(self, ap):
        self.ap = ap
        self.n = ap.shape[1]
        self.off = 0

    def reset(self):
        self.off = 0

    def f32(self, n):
        a = self.ap[:, self.off:self.off + n]
        self.off += n
        assert self.off <= self.n, "arena overflow %d > %d" % (self.off, self.n)
        return T(a)

    def bf16(self, n):
        w = (n + 1) // 2
        a = self.ap[:, self.off:self.off + w].bitcast(BF16)
        self.off += w
        assert self.off <= self.n, "arena overflow %d > %d" % (self.off, self.n)
        return T(a[:, 0:n])


COL_C = 0
COL_MODB = COL_C + 16
COL_NG = COL_MODB + 4 * 72
COL_FG = COL_NG + 96
COL_GN = COL_FG + 8
COL_LB = COL_GN + 2
COL_QN = COL_LB + 32
COL_KVN = COL_QN + 6
NCOLS = COL_KVN + 4


class K:
    def __init__(self, cfg):
        self.cfg = cfg
        self.TL = cfg.get('TL', 4096)
        self.T = self.TL + CTX
        self.depth = cfg.get('depth', 4)
        self.dump = cfg.get('dump', ())
        self.nc = bass.Bass("TRN2", target_bir_lowering=False)
        self.S = Sched()

    def din(self, name, shape, dt=F32):
        return self.nc.dram_tensor(name, list(shape), dt, kind="ExternalInput").ap()

    def dscr(self, name, shape, dt):
        kind = "ExternalOutput" if name in self.dump else "Internal"
        return self.nc.dram_tensor(name, list(shape), dt, kind=kind).ap()

    def mm(self, out, lhsT, rhs, start=True, stop=True):
        o, l, r = out.ap, lhsT.ap, rhs.ap
        self.S.add('pe', lambda e: e.matmul(o, lhsT=l, rhs=r, start=start, stop=stop),
                   [lhsT.buf, rhs.buf], [out.buf])

    def tr(self, out, in_, ident):
        o, i, d = out.ap, in_.ap, ident.ap
        self.S.add('pe', lambda e: e.transpose(o, i, d), [in_.buf, ident.buf], [out.buf])

    def act(self, out, in_, func, scale=1.0, bias=None):
        reads = [in_.buf]
        sc, bi = scale, bias
        if isinstance(scale, T):
            reads.append(scale.buf)
            sc = scale.ap
        if isinstance(bias, T):
            reads.append(bias.buf)
            bi = bias.ap
        o, i = out.ap, in_.ap
        if bi is None:
            self.S.add('act', lambda e: e.activation(out=o, in_=i, func=func, scale=sc), reads, [out.buf])
        else:
            self.S.add('act', lambda e: e.activation(out=o, in_=i, func=func, scale=sc, bias=bi), reads, [out.buf])

    def tt(self, eng, out, a, b, op):
        o, x, y = out.ap, a.ap, b.ap
        self.S.add(eng, lambda e: e.tensor_tensor(out=o, in0=x, in1=y, op=op), [a.buf, b.buf], [out.buf])

    def ts(self, eng, out, a, s1, s2, op0, op1=None):
        reads = [a.buf]
        v1, v2 = s1, s2
        if isinstance(s1, T):
            reads.append(s1.buf)
            v1 = s1.ap
        if isinstance(s2, T):
            reads.append(s2.buf)
            v2 = s2.ap
        o, x = out.ap, a.ap
        if op1 is None:
            self.S.add(eng, lambda e: e.tensor_scalar(out=o, in0=x, scalar1=v1, scalar2=None, op0=op0),
                       reads, [out.buf])
        else:
            self.S.add(eng, lambda e: e.tensor_scalar(out=o, in0=x, scalar1=v1, scalar2=v2, op0=op0, op1=op1),
                       reads, [out.buf])

    def stt(self, out, in0, scalar, in1, op0, op1):
        reads = [in0.buf, in1.buf]
        sc = scalar
        if isinstance(scalar, T):
            reads.append(scalar.buf)
            sc = scalar.ap
        o, x, y = out.ap, in0.ap, in1.ap
        self.S.add('dve', lambda e: e.scalar_tensor_tensor(out=o, in0=x, scalar=sc, in1=y, op0=op0, op1=op1),
                   reads, [out.buf])

    def copy(self, eng, out, in_):
        o, i = out.ap, in_.ap
        if eng == 'act':
            self.S.add('act', lambda e: e.copy(out=o, in_=i), [in_.buf], [out.buf])
        else:
            self.S.add(eng, lambda e: e.tensor_copy(out=o, in_=i), [in_.buf], [out.buf])

    def recip(self, out, in_):
        o, i = out.ap, in_.ap
        self.S.add('dve', lambda e: e.reciprocal(out=o, in_=i), [in_.buf], [out.buf])

    def memset(self, eng, out, val):
        o = out.ap
        self.S.add(eng, lambda e: e.memset(o, val), [], [out.buf])

    def scan(self, out, d0, d1, init, op0, op1):
        o, a, b = out.ap, d0.ap, d1.ap
        self.S.add('dve', lambda e: e.tensor_tensor_scan(out=o, data0=a, data1=b, initial=init, op0=op0, op1=op1),
                   [d0.buf, d1.buf], [out.buf])

    def dma_in(self, q, dst, src_ap, **kw):
        o = dst.ap
        self.S.add(q, lambda e: e.dma_start(out=o, in_=src_ap, **kw), [], [dst.buf], dma=True)

    def dma_out(self, q, dst_ap, src, **kw):
        i = src.ap
        self.S.add(q, lambda e: e.dma_start(out=dst_ap, in_=i, **kw), [src.buf], [], dma=True)

    def wload(self, dst, src_ap):
        self.dma_in('pool', dst, src_ap, max_dma_last_dim=4096)

    def phase_end(self):
        self.S.barrier()
        self.arena.reset()

    def tiles(self, with_ctx=True):
        out = [(0, CTX, 1)] if with_ctx else []
        out += [(CTX + 512 * i, 512, 0) for i in range(self.TL // 512)]
        return out

    def col(self, c):
        return self.cols[:, c:c + 1]

    def modc(self, i, m, ch, w):
        c = ((i * NMOD + m) * NCH + ch) * 2 + w
        return self.MODC[:, c:c + 1]

    def gmc(self, i, s_, ch, w):
        c = ((i * 3 + s_) * NCH + ch) * 2 + w
        return self.GM[:, c:c + 1]

    def hgc(self, i, s_, ch, w):
        c = ((i * 3 + s_) * NCH + ch) * 2 + w
        return self.HG[:, c:c + 1]

    def phase_consts(self):
        s = self
        s.dma_in('sp', s.cols, s.d_cols)
        s.dma_in('sp', s.identf, s.d_consts[:, 0:128])
        cf = s.arena.f32(384)
        s.dma_in('sp', cf, s.d_consts)
        s.copy('dve', s.cst, cf)
        s.memset('pool', s.ones, 1.0)
        s.memset('pool', s.LB[:, 0:16], 0.0)
        dl = s.arena.f32(16)
        s.tt('dve', dl, s.cols[:, COL_LB + 16:COL_LB + 32], s.cols[:, COL_LB:COL_LB + 16], ALU.subtract)
        s.act(s.LB[:, 16:32], dl, AF.Sigmoid)
        s.ts('dve', s.OML, s.LB, -1.0, 1.0, ALU.mult, ALU.add)
        s.phase_end()

    def mod_derive(self, i):
        s = self
        for s_ in range(3):
            for w in range(2):
                b_sc = ((i * NMOD + 3 * s_ + 1) * NCH) * 2
                b_gt = ((i * NMOD + 3 * s_ + 2) * NCH) * 2
                b_o = ((i * 3 + s_) * NCH) * 2
                ng = s.cols[:, COL_NG + (i * 3 + s_) * 8:COL_NG + (i * 3 + s_) * 8 + 8]
                s.stt(s.GM[:, b_o + w:b_o + 16:2], s.MODC[:, b_sc + w:b_sc + 16:2], 1.0, ng,
                      ALU.add, ALU.mult)
                s.ts('dve', s.HG[:, b_o + w:b_o + 16:2], s.MODC[:, b_gt + w:b_gt + 16:2],
                     0.5 if s_ != 1 else 1.0, None, ALU.mult)

    def mod_bg_steps(self, layers, A):
        s = self
        Wb = [[A.bf16(1024) for _ in range(8)] for _ in range(2)]
        rowb = A.f32(1024)
        blocks = [(i, m) for i in layers for m in range(NMOD)]
        steps = []

        def stepA(bi):
            i, m = blocks[bi]
            for k in range(8):
                s.wload(Wb[bi % 2][k], s.d_mod_w[i, k * 128:(k + 1) * 128, m * 1024:(m + 1) * 1024])

        def stepB(bi):
            i, m = blocks[bi]
            W = Wb[bi % 2]
            for half in range(2):
                pb = s.ps[6 + half]
                for k in range(8):
                    s.mm(pb[0:2, :], s.scbf[:, k, :], W[k][:, half * 512:(half + 1) * 512], k == 0, k == 7)
                s.copy('act', rowb[0:2, half * 512:(half + 1) * 512], pb[0:2, :])
            pt = s.ps[6]
            for ch in range(8):
                s.mm(pt[:, ch * 2:ch * 2 + 2], rowb[0:2, ch * 128:(ch + 1) * 128], s.identf[0:2, 0:2])
            base = ((i * NMOD + m) * NCH) * 2
            for w in range(2):
                s.tt('dve', s.MODC[:, base + w:base + 16:2], pt[:, w:16:2],
                     s.cols[:, COL_MODB + i * 72 + m * 8:COL_MODB + i * 72 + m * 8 + 8], ALU.add)
            if m == NMOD - 1:
                s.mod_derive(i)

        for bi in range(len(blocks)):
            def st(bi=bi):
                if bi == 0:
                    stepA(0)
                if bi + 1 < len(blocks):
                    stepA(bi + 1)
                stepB(bi)
            steps.append(st)
        return steps

    def phase_mod(self, layers):
        s = self
        A = s.arena
        sc = A.f32(16)
        s.act(sc, s.cols[:, COL_C:COL_C + 16], AF.Silu)
        sc3 = sc.re("p (w c) -> p c w", w=2)
        s.copy('dve', s.scbf, sc3)
        wb = [[T(A.f32(1024).ap) for _ in range(8)] for _ in range(2)]
        blk = 0
        for i in layers:
            for m in range(NMOD):
                W = wb[blk % 2]
                for k in range(8):
                    s.dma_in('sp' if k % 2 == 0 else 'act', W[k],
                             s.d_mod_w[i, k * 128:(k + 1) * 128, m * 1024:(m + 1) * 1024])
                ps = s.ps[blk % 2]
                for ch in range(8):
                    for k in range(8):
                        s.mm(ps[:, ch * 2:ch * 2 + 2], W[k][:, ch * 128:(ch + 1) * 128], sc3[:, k, :],
                             k == 0, k == 7)
                base = ((i * NMOD + m) * NCH) * 2
                for w in range(2):
                    outv = s.MODC[:, base + w:base + 16:2]
                    s.tt('dve', outv, ps[:, w:16:2],
                         s.cols[:, COL_MODB + i * 72 + m * 8:COL_MODB + i * 72 + m * 8 + 8], ALU.add)
                blk += 1
            s.mod_derive(i)
        s.phase_end()

    def norm_mod(self, h3, a3, n, gm, sh, sq, rt, rstd, tmps, D_=D, nch=NCH, out_scale=None, stage=0):
        s = self
        if stage in (0, 1):
            s.act(sq[:, :, :n], h3[:, :, :n], AF.Square)
        if stage == 1:
            return
        pst = s.ps[6]
        if stage in (0, 2):
            for ch in range(nch):
                s.mm(pst[:, :n], s.ones, sq[:, ch, :n], ch == 0, ch == nch - 1)
            s.act(rt[:, :n], pst[:, :n], AF.Sqrt, scale=1.0 / D_, bias=s.epsc)
            s.recip(rstd[:, :n], rt[:, :n])
        if stage == 2:
            return
        for ch in (range(nch) if stage == 0 else [stage - 3]):
            tm = tmps[ch % 2]
            s.tt('dve' if ch % 2 == 0 else 'pool', tm[:, :n], h3[:, ch, :n], rstd[:, :n], ALU.mult)
            if sh is None:
                s.act(a3[:, ch, :n], tm[:, :n], AF.Copy, scale=gm(ch))
            else:
                s.act(a3[:, ch, :n], tm[:, :n], AF.Identity, scale=gm(ch), bias=sh(ch))

    def hview(self, dram, t0, n):
        return dram.rearrange("(c p) t -> p c t", p=128)[:, :, t0:t0 + n]

    def ffn_wsets(self):
        A = self.arena
        if not hasattr(self, '_wsets'):
            assert A.off == 0
            self._wsets = []
            for b in range(2):
                Wg = [A.bf16(1024) for _ in range(8)]
                Wu = [A.bf16(1024) for _ in range(8)]
                Wd = [A.bf16(1024) for _ in range(8)]
                self._wsets.append((Wg, Wu, Wd))
            self._wset_end = A.off
        A.off = self._wset_end
        return self._wsets

    def ffn_wload(self, desc, setidx):
        s = self
        (i, j, fa, fb) = desc[:4]
        nf = fb - fa
        Wg, Wu, Wd = s._wsets[setidx]
        for k in range(8):
            s.wload(Wg[k][:, :nf * 128], s.d_wg[i, j, k * 128:(k + 1) * 128, fa * 128:fb * 128])
            s.wload(Wu[k][:, :nf * 128], s.d_wu[i, j, k * 128:(k + 1) * 128, fa * 128:fb * 128])
        for f in range(nf):
            s.wload(Wd[f], s.d_wd[i, j, (fa + f) * 128:(fa + f + 1) * 128, :])

    def ffn_pass(self, desc, pidx, prefetched, nxt):
        s = self
        A = s.arena
        (i, j, fa, fb, first, src, dst, with_ctx) = desc
        nf = fb - fa
        s_ = 0 if j == 0 else 2
        Wg, Wu, Wd = s.ffn_wsets()[pidx % 2]
        if not prefetched:
            s.ffn_wload(desc, pidx % 2)
        if nxt is not None:
            s.ffn_wload(nxt, (pidx + 1) % 2)
        hb = [A.f32(8 * 512).re("p (c t) -> p c t", c=8) for _ in range(2)]
        ab = [A.bf16(8 * 512).re("p (c t) -> p c t", c=8) for _ in range(2)]
        hm = [[A.bf16(512) for _ in range(nf)] for _ in range(2)]
        sg = [A.f32(512) for _ in range(2)]
        if first:
            sq = A.bf16(8 * 512).re("p (c t) -> p c t", c=8)
            rt = A.f32(512)
            rstd = A.f32(512)
            tmps = [A.f32(512) for _ in range(2)]
        tiles = s.tiles(with_ctx)

        def load(ti):
            t0, n, w = tiles[ti]
            s.dma_in('sp', hb[ti % 2][:, :, :n], s.hview(src if first else dst, t0, n))
            if not first:
                s.dma_in('sp', ab[ti % 2][:, :, :n], s.hview(s.d_AT, t0, n))

        def norm(ti, stage):
            t0, n, w = tiles[ti]
            s.norm_mod(hb[ti % 2], ab[ti % 2], n, lambda ch: s.gmc(i, s_, ch, w),
                       lambda ch: s.modc(i, 3 * s_, ch, w), sq, rt, rstd, tmps, stage=stage)
            if stage in (0, 10):
                s.dma_out('sp', s.hview(s.d_AT, t0, n), ab[ti % 2][:, :, :n])

        load(0)
        if first:
            norm(0, 0)
        for ti, (t0, n, w) in enumerate(tiles):
            if ti + 1 < len(tiles):
                load(ti + 1)
            b = ti % 2
            h3, a3 = hb[b], ab[b]
            for f in range(nf):
                if first and ti + 1 < len(tiles):
                    if f == nf // 2:
                        norm(ti + 1, 1)
                    if f == nf // 2 + 2:
                        norm(ti + 1, 2)
                pg, pu = s.ps[f % 2], s.ps[2 + f % 2]
                for k in range(8):
                    s.mm(pg[:, :n], Wg[k][:, f * 128:(f + 1) * 128], a3[:, k, :n], k == 0, k == 7)
                for k in range(8):
                    s.mm(pu[:, :n], Wu[k][:, f * 128:(f + 1) * 128], a3[:, k, :n], k == 0, k == 7)
                s.act(sg[f % 2][:, :n], pg[:, :n], AF.Silu)
                s.tt('dve', hm[b][f][:, :n], sg[f % 2][:, :n], pu[:, :n], ALU.mult)
            for dch in range(8):
                if first and ti + 1 < len(tiles):
                    norm(ti + 1, 3 + dch)
                py = s.ps[4 + dch % 2]
                for f in range(nf):
                    s.mm(py[:, :n], Wd[f][:, dch * 128:(dch + 1) * 128], hm[b][f][:, :n], f == 0, f == nf - 1)
                s.stt(h3[:, dch, :n], py[:, :n], s.hgc(i, s_, dch, w), h3[:, dch, :n], ALU.mult, ALU.add)
            s.dma_out('sp', s.hview(dst, t0, n), h3[:, :, :n])
        s.phase_end()

    def ffn_descs(self, i, j, src, dst, with_ctx=True):
        splits = self.cfg.get('fsplits', [(0, 8), (8, 15), (15, 22)])
        return [('ffn', (i, j, fa, fb, si == 0, src, dst, with_ctx)) for si, (fa, fb) in enumerate(splits)]

    def phase_final(self, src):
        s = self
        A = s.arena
        hb = [A.f32(8 * 512).re("p (c t) -> p c t", c=8) for _ in range(2)]
        ob = [A.f32(8 * 512).re("p (c t) -> p c t", c=8) for _ in range(2)]
        sq = A.bf16(8 * 512).re("p (c t) -> p c t", c=8)
        rt = A.f32(512)
        rstd = A.f32(512)
        tiles = s.tiles(False)
        s.dma_in('sp', hb[0], s.hview(src, tiles[0][0], 512))
        for ti, (t0, n, w) in enumerate(tiles):
            if ti + 1 < len(tiles):
                s.dma_in('sp', hb[(ti + 1) % 2], s.hview(src, tiles[ti + 1][0], 512))
            h3, o3 = hb[ti % 2], ob[ti % 2]
            s.act(sq, h3, AF.Square)
            pst = s.ps[6]
            for ch in range(8):
                s.mm(pst, s.ones, sq[:, ch, :], ch == 0, ch == 7)
            s.act(rt, pst, AF.Sqrt, scale=1.0 / D, bias=s.epsc)
            s.recip(rstd, rt)
            for ch in range(8):
                s.stt(o3[:, ch, :], h3[:, ch, :], s.col(COL_FG + ch), rstd, ALU.mult, ALU.mult)
            s.dma_out('sp', s.hview(s.d_out, t0 - CTX, 512), o3)
        s.phase_end()


    def lbc(self, j, d, h):
        c = j * 16 + d * 8 + h
        return self.LB[:, c:c + 1], self.OML[:, c:c + 1]

    def hgrn_proj(self, i, src):
        s = self
        A = s.arena
        j = i // 2
        jl = s.cfg.get('lb_layer', j)
        Win = [A.bf16(4096) for _ in range(8)]
        for k in range(8):
            for c5, c4 in ((0, 0), (1, 1), (2, 2), (4, 3)):
                s.wload(Win[k][:, c4 * 1024:(c4 + 1) * 1024],
                        s.d_hg_in[j, k * 128:(k + 1) * 128, c5 * 1024:(c5 + 1) * 1024])
        hb = A.f32(8 * 512).re("p (c t) -> p c t", c=8)
        a3 = A.bf16(8 * 512).re("p (c t) -> p c t", c=8)
        sq = A.bf16(8 * 512).re("p (c t) -> p c t", c=8)
        rt = A.f32(512)
        rstd = A.f32(512)
        tmps = [A.f32(512) for _ in range(2)]
        qsb = [A.f32(512) for _ in range(2)]
        tb = [[A.f32(512) for _ in range(6)] for _ in range(4)]
        ob = [[[A.bf16(512) for _ in range(3)] for _ in range(4)] for _ in range(2)]
        ecl = [[A.f32(16) for _ in range(4)] for _ in range(2)]
        gst = [A.f32(512) for _ in range(2)]
        rmF = A.f32(512)
        rmB = A.f32(512)
        s.memset('pool', rmF, 1.0)
        s.memset('pool', rmB, 1.0)
        s.memset('pool', rmF.re("p (c j) -> p c j", j=32)[:, :, 0:1], 0.0)
        s.memset('pool', rmB.re("p (c j) -> p c j", j=32)[:, :, 31:32], 0.0)
        tiles = s.tiles(True)
        grp = 0
        gcn = 0
        for ti, (t0, n, w) in enumerate(tiles):
            nc32 = n // 32
            s.dma_in('sp', hb[:, :, :n], s.hview(src, t0, n))
            s.norm_mod(hb, a3, n, lambda ch: s.gmc(i, 1, ch, w), lambda ch: s.modc(i, 3, ch, w),
                       sq, rt, rstd, tmps)
            s.dma_out('sp', s.hview(s.d_AT, t0, n), a3[:, :, :n])
            for hp in range(4):
                chains = []
                for h2 in range(2):
                    h = hp * 2 + h2
                    pq = s.ps[h2 * 3]
                    for k in range(8):
                        s.mm(pq[:, :n], Win[k][:, h * 128:(h + 1) * 128], a3[:, k, :n], k == 0, k == 7)
                    q = qsb[h2]
                    s.copy('act', q[:, :n], pq[:, :n])
                    for d in range(2):
                        pz = s.ps[h2 * 3 + 1 + d]
                        c0 = 1024 + d * 1024 + h * 128
                        for k in range(8):
                            s.mm(pz[:, :n], Win[k][:, c0:c0 + 128], a3[:, k, :n], k == 0, k == 7)
                        chains.append((h, d, q, pz, len(chains)))
                bset = grp % 2
                grp += 1
                V_ = {}
                for (h, d, q, pz, ci) in chains:
                    V_[ci] = [x[:, :n] for x in tb[ci]]
                for (h, d, q, pz, ci) in chains:
                    tA = V_[ci][0]
                    s.act(tA, pz[:, :n], AF.Sigmoid)
                for (h, d, q, pz, ci) in chains:
                    tA = V_[ci][0]
                    lb, oml = s.lbc(jl, d, h)
                    s.act(tA, tA, AF.Identity, scale=oml, bias=lb)
                for (h, d, q, pz, ci) in chains:
                    tA, tB, tK = V_[ci][0], V_[ci][1], V_[ci][2]
                    s.ts('dve', tB, tA, F_MIN, None, ALU.max)
                    s.ts('pool', tK, tA, -1.0, 1.0, ALU.mult, ALU.add)
                for (h, d, q, pz, ci) in chains:
                    tB = V_[ci][1]
                    s.act(tB, tB, AF.Ln)
                for (h, d, q, pz, ci) in chains:
                    tB, tC = V_[ci][1], V_[ci][3]
                    if d == 0:
                        s.scan(tC, rmF[:, :n], tB, 0.0, ALU.mult, ALU.add)
                    else:
                        s.scan(tC[:, ::-1], rmB[:, :n][:, ::-1], tB[:, ::-1], 0.0, ALU.mult, ALU.add)
                for (h, d, q, pz, ci) in chains:
                    tC, tE = V_[ci][3], V_[ci][5]
                    c3 = tC.re("p (c j) -> p c j", j=32)
                    e = 31 if d == 0 else 0
                    s.tt('dve' if ci % 2 == 0 else 'pool', tE.re("p (c j) -> p c j", j=32),
                         c3[:, :, e:e + 1].bc([128, nc32, 32]), c3, ALU.subtract)
                for (h, d, q, pz, ci) in chains:
                    tA, tC, tD, tE = V_[ci][0], V_[ci][3], V_[ci][4], V_[ci][5]
                    s.act(tA, tC, AF.Exp)
                    s.act(tD, tC, AF.Exp, scale=-1.0)
                    s.act(tE, tE, AF.Exp)
                for (h, d, q, pz, ci) in chains:
                    tA, tK, tD, tE = V_[ci][0], V_[ci][2], V_[ci][4], V_[ci][5]
                    oQ, oK, oE = [x[:, :n] for x in ob[bset][ci]]
                    ec = ecl[bset][ci]
                    e = 31 if d == 0 else 0
                    s.tt('dve', oQ, q[:, :n], tA, ALU.mult)
                    s.tt('pool', oK, tK, tD, ALU.mult)
                    s.tt('dve', oE, tK, tE, ALU.mult)
                    s.copy('dve', ec[:, :nc32], tA.re("p (c j) -> p c j", j=32)[:, :, e])
                    s.dma_out('sp', s.d_GQ[d, h, :, t0:t0 + n], oQ)
                    s.dma_out('sp', s.d_GK[d, h, :, t0:t0 + n], oK)
                    s.dma_out('sp', s.d_GE[d, h, :, t0:t0 + n], oE)
                    s.dma_out('sp', s.d_GD[d, h, :, t0 // 32:t0 // 32 + nc32], ec[:, :nc32])
            for gc in range(8):
                pg = s.ps[gc % 4]
                for k in range(8):
                    s.mm(pg[:, :n], Win[k][:, 3072 + gc * 128:3072 + (gc + 1) * 128], a3[:, k, :n], k == 0, k == 7)
                gb = gst[gcn % 2]
                gcn += 1
                s.act(gb[:, :n], pg[:, :n], AF.Silu)
                s.dma_out('sp', s.d_GS[gc * 128:(gc + 1) * 128, t0:t0 + n], gb[:, :n])
        s.phase_end()

    def hgrn_v(self, i):
        s = self
        A = s.arena
        j = i // 2
        Wv = [A.bf16(1024) for _ in range(8)]
        for k in range(8):
            s.wload(Wv[k], s.d_hg_in[j, k * 128:(k + 1) * 128, 3072:4096])
        ab = [A.bf16(8 * 512).re("p (c t) -> p c t", c=8) for _ in range(2)]
        vt = [A.bf16(1024) for _ in range(4)]
        tiles = s.tiles(True)
        s.dma_in('sp', ab[0][:, :, :tiles[0][1]], s.hview(s.d_AT, tiles[0][0], tiles[0][1]))
        pc = 0
        vc = 0
        for ti, (t0, n, w) in enumerate(tiles):
            if ti + 1 < len(tiles):
                t1, n1, _ = tiles[ti + 1]
                s.dma_in('sp', ab[(ti + 1) % 2][:, :, :n1], s.hview(s.d_AT, t1, n1))
            a3 = ab[ti % 2]
            for sub in range(n // 128):
                vtt = vt[vc % 4]
                vc += 1
                for cg in range(2):
                    p = s.ps[pc % 6]
                    pc += 1
                    for k in range(8):
                        s.mm(p, a3[:, k, sub * 128:(sub + 1) * 128], Wv[k][:, cg * 512:(cg + 1) * 512], k == 0, k == 7)
                    s.copy('dve' if cg == 0 else 'act', vtt[:, cg * 512:(cg + 1) * 512], p)
                s.dma_out('sp', s.d_V[t0 + sub * 128:t0 + (sub + 1) * 128, :], vtt)
        s.phase_end()

    def hgrn_scan(self, d):
        s = self
        A = s.arena
        fwd = d == 0
        G = 2

        def mk(fn):
            return [[fn() for _ in range(G)] for _ in range(2)]
        Q = mk(lambda: A.bf16(4 * 512).re("p (h t) -> p h t", h=4))
        Kt = mk(lambda: A.bf16(4 * 512).re("p (h t) -> p h t", h=4))
        Ke = mk(lambda: A.bf16(4 * 512).re("p (h t) -> p h t", h=4))
        Vt = mk(lambda: A.bf16(4 * 512).re("p (j h v) -> p j h v", j=4, h=4))
        V32 = mk(lambda: A.bf16(16 * 512).re("p (c h v) -> p c h v", c=16, h=4))
        EC = mk(lambda: A.f32(4 * 16).re("p (h c) -> p h c", h=4))
        ost = mk(lambda: A.f32(4 * 512).re("p (h t) -> p h t", h=4))
        Sf = [A.f32(4 * 128).re("p (h v) -> p h v", h=4) for _ in range(G)]
        Sb = [A.bf16(4 * 128).re("p (h v) -> p h v", h=4) for _ in range(G)]
        At = [A.bf16(4 * 128).re("p (h t) -> p h t", h=4) for _ in range(G)]
        KEt = [[A.bf16(4 * 512).re("p (h c k) -> p h c k", h=4, c=4) for _ in range(G)] for _ in range(2)]
        for g in range(G):
            s.memset('pool', Sf[g], 0.0)
            s.memset('pool', Sb[g], 0.0)
        mask = s.maskF if fwd else s.maskB
        mask4 = T(mask.ap.unsqueeze(1).to_broadcast([128, 4, 128]), mask.buf)
        scP = [s.ps[0].re("p (h t) -> p h t", h=4), s.ps[1].re("p (h t) -> p h t", h=4)]
        inP = s.ps[2].re("p (h t) -> p h t", h=4)
        itP = [s.ps[3].re("p (h t) -> p h t", h=4), s.ps[4].re("p (h t) -> p h t", h=4)]
        pP = [s.ps[5].re("p (h v) -> p h v", h=4), s.ps[6].re("p (h v) -> p h v", h=4)]
        trP = T(s.ps[7].ap.bitcast(BF16), s.ps[7].buf)
        tiles = s.tiles(True)
        order = tiles if fwd else [tiles[0]] + tiles[:0:-1]
        dst = s.d_OF if fwd else s.d_OB

        def load(si):
            t0, n, w = order[si]
            b = si % 2
            for g in range(G):
                q_ = 'sp' if g == 0 else 'act'
                hs = slice(4 * g, 4 * g + 4)
                cs_ = slice(512 * g, 512 * g + 512)
                s.dma_in(q_, Q[b][g][:, :, :n], s.d_GQ[d, hs, :, t0:t0 + n].rearrange("h p t -> p h t"))
                s.dma_in(q_, Kt[b][g][:, :, :n], s.d_GK[d, hs, :, t0:t0 + n].rearrange("h p t -> p h t"))
                s.dma_in(q_, Ke[b][g][:, :, :n], s.d_GE[d, hs, :, t0:t0 + n].rearrange("h p t -> p h t"))
                s.dma_in(q_, Vt[b][g][:, :n // 128].re("p j h v -> p j (h v)"),
                         s.d_V[t0:t0 + n, cs_].rearrange("(j p) c -> p j c", p=128))
                s.dma_in(q_, V32[b][g][0:32, :n // 32].re("p c h v -> p c (h v)"),
                         s.d_V[t0:t0 + n, cs_].rearrange("(c p) x -> p c x", p=32))
                s.dma_in(q_, EC[b][g][:, :, :n // 32],
                         s.d_GD[d, hs, :, t0 // 32:(t0 + n) // 32].rearrange("h p c -> p h c"))

        load(0)
        kc = 0
        for si, (t0, n, w) in enumerate(order):
            if si + 1 < len(order):
                load(si + 1)
            b = si % 2
            njt = n // 128
            jts = list(range(njt)) if fwd else list(range(njt - 1, -1, -1))
            cs_ = [0, 1, 2, 3] if fwd else [3, 2, 1, 0]
            for jt in jts:
                ts0 = jt * 128
                kets = []
                for g in range(G):
                    for h in range(4):
                        s.mm(scP[g][:, h, :], Kt[b][g][:, h, ts0:ts0 + 128], Q[b][g][:, h, ts0:ts0 + 128])
                    s.tt('dve', At[g], scP[g], mask4, ALU.mult)
                    ke = KEt[kc % 2][g]
                    for hp in range(2):
                        for h2 in range(2):
                            h = hp * 2 + h2
                            for c in range(4):
                                o0 = (h2 * 4 + c) * 128
                                s.tr(trP[0:32, o0:o0 + 128], Ke[b][g][:, h, ts0 + c * 32:ts0 + (c + 1) * 32], s.ident)
                        s.copy('act', ke[0:32, 2 * hp:2 * hp + 2].re("p h c k -> p (h c k)"), trP[0:32, :])
                    kets.append(ke)
                    for h in range(4):
                        s.mm(inP[:, h, :], Vt[b][g][:, jt, h, :], At[g][:, h, :])
                    s.copy('act', ost[b][g][:, :, ts0:ts0 + 128], inP)
                kc += 1
                for c in cs_:
                    ch = jt * 4 + c
                    for g in range(G):
                        for h in range(4):
                            s.mm(itP[g][:, h, c * 32:(c + 1) * 32], Sb[g][:, h, :],
                                 Q[b][g][:, h, ts0 + c * 32:ts0 + (c + 1) * 32])
                        for h in range(4):
                            s.mm(pP[g][:, h, :], kets[g][0:32, h, c, :], V32[b][g][0:32, ch, h, :])
                        s.tt('dve' if g == 0 else 'pool', Sf[g], Sf[g],
                             EC[b][g][:, :, ch:ch + 1].bc([128, 4, 128]), ALU.mult)
                        s.tt('dve', Sf[g], Sf[g], pP[g], ALU.add)
                        s.copy('act', Sb[g], Sf[g])
                for g in range(G):
                    o_ = ost[b][g][:, :, ts0:ts0 + 128]
                    s.tt('dve', o_, o_, itP[g], ALU.add)
            for g in range(G):
                s.dma_out('sp', dst.rearrange("(h p) t -> p h t", p=128)[:, 4 * g:4 * g + 4, t0:t0 + n],
                          ost[b][g][:, :, :n])
        s.phase_end()

    def hgrn_post(self, i, with_ctx):
        s = self
        A = s.arena
        j = i // 2
        of = [A.f32(8 * 512).re("p (c t) -> p c t", c=8) for _ in range(2)]
        ob_ = [A.f32(8 * 512).re("p (c t) -> p c t", c=8) for _ in range(2)]
        gs = [A.f32(8 * 512).re("p (c t) -> p c t", c=8) for _ in range(2)]
        sq = A.bf16(8 * 512).re("p (c t) -> p c t", c=8)
        rt = A.f32(8 * 512).re("p (c t) -> p c t", c=8)
        r3 = [A.bf16(8 * 512).re("p (c t) -> p c t", c=8) for _ in range(2)]
        tiles = s.tiles(with_ctx)
        psall = T(s.psum_all.rearrange("p (c t) -> p c t", c=8), tuple(p.buf for p in s.ps))

        def load(ti):
            t0, n, w = tiles[ti]
            s.dma_in('sp', of[ti % 2][:, :, :n], s.hview(s.d_OF, t0, n))
            s.dma_in('act', ob_[ti % 2][:, :, :n], s.hview(s.d_OB, t0, n))
            s.dma_in('sp', gs[ti % 2][:, :, :n], s.hview(s.d_GS, t0, n))

        load(0)
        for ti, (t0, n, w) in enumerate(tiles):
            if ti + 1 < len(tiles):
                load(ti + 1)
            b = ti % 2
            o3 = of[b]
            s.tt('pool', o3[:, :, :n], o3[:, :, :n], ob_[b][:, :, :n], ALU.add)
            s.act(sq[:, :, :n], o3[:, :, :n], AF.Square)
            for h in range(8):
                s.mm(s.ps[h][:, :n], s.ones, sq[:, h, :n])
            s.act(rt[:, :, :n], psall[:, :, :n], AF.Sqrt, scale=1.0 / 128, bias=s.epsc)
            s.recip(rt[:, :, :n], rt[:, :, :n])
            s.tt('dve', o3[:, :, :n], o3[:, :, :n], rt[:, :, :n], ALU.mult)
            s.stt(r3[b][:, :, :n], o3[:, :, :n], s.col(COL_GN + j), gs[b][:, :, :n], ALU.mult, ALU.mult)
            s.dma_out('sp', s.hview(s.d_OT, t0, n), r3[b][:, :, :n])
        s.phase_end()

    def hgrn_layer(self, i, with_ctx, pidx=0, nxt=None):
        st = self.cfg.get('hg_stop', 5)
        self.hgrn_proj(i, self.d_H)
        self.hgrn_v(i)
        if st >= 2:
            self.hgrn_scan(0)
        if st >= 3:
            self.hgrn_scan(1)
        if st >= 4:
            self.hgrn_post(i, with_ctx)
        if st >= 5:
            self.outproj(self.d_hg_out[i // 2], i, with_ctx, pidx, nxt)

    def rot_cols(self, dst, src):
        s = self
        d4 = dst.re("p g (a h i) -> p g a h i", a=2, h=2)
        s4 = src.re("p g (a h i) -> p g a h i", a=2, h=2)
        for a in range(2):
            s.ts('pool', d4[:, :, a, 0, :], s4[:, :, a, 1, :], -1.0, None, ALU.mult)
            s.copy('pool', d4[:, :, a, 1, :], s4[:, :, a, 0, :])

    def mla_proj(self, i, src):
        s = self
        A = s.arena
        j = i // 2
        Wdn = [A.bf16(704) for _ in range(8)]
        Wdr = [A.bf16(64) for _ in range(8)]
        Wuq = [A.bf16(1536) for _ in range(3)]
        Wqr = [A.bf16(512) for _ in range(3)]
        Wkv = [A.bf16(2048) for _ in range(2)]
        for k in range(8):
            s.wload(Wdn[k], s.d_mla_down[j, k * 128:(k + 1) * 128, :])
        for k in range(3 if 'wq' not in s.cfg.get('mp_skip', ()) else 0):
            s.wload(Wuq[k], s.d_mla_uq[j, k * 128:(k + 1) * 128, :])
        for k in range(2 if 'wkv' not in s.cfg.get('mp_skip', ()) else 0):
            s.wload(Wkv[k], s.d_mla_ukv[j, k * 128:(k + 1) * 128, :])
        for k in range(8 if 'rot' not in s.cfg.get('mp_skip', ()) else 0):
            s.rot_cols(Wdr[k].re("p (g c) -> p g c", g=1), Wdn[k][:, 640:704].re("p (g c) -> p g c", g=1))
        for k in range(3 if 'rot' not in s.cfg.get('mp_skip', ()) else 0):
            s.rot_cols(Wqr[k].re("p (g c) -> p g c", g=8), Wuq[k].re("p (g c) -> p g c", g=8)[:, :, 128:192])
        hb = [A.f32(8 * 512).re("p (c t) -> p c t", c=8) for _ in range(2)]
        a3 = A.bf16(8 * 512).re("p (c t) -> p c t", c=8)
        sq = A.bf16(8 * 512).re("p (c t) -> p c t", c=8)
        rt = A.f32(512)
        rstd = A.f32(512)
        tmps = [A.f32(512) for _ in range(2)]
        cqf = [A.f32(512) for _ in range(5)]
        sqd = A.bf16(5 * 512).re("p (c t) -> p c t", c=5)
        cqn = [A.bf16(512) for _ in range(5)]
        rt2 = [A.f32(512) for _ in range(2)]
        rs2 = [A.f32(512) for _ in range(2)]
        cs = [A.f32(2 * 512).re("p (c t) -> p c t", c=2) for _ in range(2)]
        r1 = [A.f32(512) for _ in range(2)]
        r2 = [A.f32(512) for _ in range(2)]
        krb = [A.bf16(512) for _ in range(2)]
        qn_all = [A.bf16(8 * 512).re("p (h t) -> p h t", h=8) for _ in range(2)]
        qr_all = [A.bf16(8 * 512).re("p (h t) -> p h t", h=8) for _ in range(2)]
        kn_all = [A.bf16(8 * 512).re("p (h t) -> p h t", h=8) for _ in range(2)]
        vt = [A.bf16(1024) for _ in range(2)]
        tiles = s.tiles(True)
        rope3 = s.d_rope.rearrange("c p t -> p c t")

        def load(ti):
            t0, n, w = tiles[ti]
            s.dma_in('sp', hb[ti % 2][:, :, :n], s.hview(src, t0, n))
            if 'rope' not in s.cfg.get('mp_skip', ()):
                s.dma_in('sp', cs[ti % 2][0:64, :, :n], rope3[:, :, t0:t0 + n])

        load(0)
        pc = 0
        for ti, (t0, n, w) in enumerate(tiles):
            if ti + 1 < len(tiles):
                load(ti + 1)
            b = ti % 2
            h3 = hb[b]
            cosT, sinT = cs[b][0:64, 0, :n], cs[b][0:64, 1, :n]
            s.norm_mod(h3, a3, n, lambda ch: s.gmc(i, 1, ch, w), lambda ch: s.modc(i, 3, ch, w),
                       sq, rt, rstd, tmps)
            for oc in range(5):
                p = s.ps[pc % 4]
                pc += 1
                for k in range(8):
                    s.mm(p[:, :n], Wdn[k][:, oc * 128:(oc + 1) * 128], a3[:, k, :n], k == 0, k == 7)
                s.copy('dve', cqf[oc][:, :n], p[:, :n])
                s.act(sqd[:, oc, :n], cqf[oc][:, :n], AF.Square)
            SK_ = s.cfg.get('mp_skip', ())
            pk, pr = s.ps[4], s.ps[5]
            if 'kr' in SK_:
                continue
            for k in range(8):
                s.mm(pk[0:64, :n], Wdn[k][:, 640:704], a3[:, k, :n], k == 0, k == 7)
            for k in range(8):
                s.mm(pr[0:64, :n], Wdr[k], a3[:, k, :n], k == 0, k == 7)
            s.tt('dve', r1[0][0:64, :n], pk[0:64, :n], cosT, ALU.mult)
            s.tt('dve', r2[0][0:64, :n], pr[0:64, :n], sinT, ALU.mult)
            s.tt('pool', krb[b][0:64, :n], r1[0][0:64, :n], r2[0][0:64, :n], ALU.add)
            s.dma_out('sp', s.d_KR[:, t0:t0 + n], krb[b][0:64, :n])
            if 'norm' in SK_:
                continue
            for g, (c0, c1, dd, colb) in enumerate(((0, 3, 384, COL_QN + j * 3), (3, 5, 256, COL_KVN + j * 2))):
                pst = s.ps[6 + g]
                for c in range(c0, c1):
                    s.mm(pst[:, :n], s.ones, sqd[:, c, :n], c == c0, c == c1 - 1)
                s.act(rt2[g][:, :n], pst[:, :n], AF.Sqrt, scale=1.0 / dd, bias=s.epsc)
                s.recip(rs2[g][:, :n], rt2[g][:, :n])
                for c in range(c0, c1):
                    s.stt(cqn[c][:, :n], cqf[c][:, :n], s.col(colb + c - c0), rs2[g][:, :n], ALU.mult, ALU.mult)
            if 'q' in SK_:
                continue
            for h in range(8):
                p = s.ps[pc % 4]
                pc += 1
                for c in range(3):
                    s.mm(p[:, :n], Wuq[c][:, h * 192:h * 192 + 128], cqn[c][:, :n], c == 0, c == 2)
                s.copy('act', qn_all[b][:, h, :n], p[:, :n])
                pk, pr = s.ps[4], s.ps[5]
                for c in range(3):
                    s.mm(pk[0:64, :n], Wuq[c][:, h * 192 + 128:h * 192 + 192], cqn[c][:, :n], c == 0, c == 2)
                for c in range(3):
                    s.mm(pr[0:64, :n], Wqr[c][:, h * 64:(h + 1) * 64], cqn[c][:, :n], c == 0, c == 2)
                s.tt('dve', r1[h % 2][0:64, :n], pk[0:64, :n], cosT, ALU.mult)
                s.tt('dve', r2[h % 2][0:64, :n], pr[0:64, :n], sinT, ALU.mult)
                s.tt('pool', qr_all[b][0:64, h, :n], r1[h % 2][0:64, :n], r2[h % 2][0:64, :n], ALU.add)
            s.dma_out('sp', s.d_QN.rearrange("h p t -> p h t")[:, :, t0:t0 + n], qn_all[b][:, :, :n])
            s.dma_out('sp', s.d_QR.rearrange("h p t -> p h t")[:, :, t0:t0 + n], qr_all[b][0:64, :, :n])
            if 'kn' in SK_:
                continue
            for h in range(8):
                p = s.ps[pc % 4]
                pc += 1
                for c in range(2):
                    s.mm(p[:, :n], Wkv[c][:, h * 256:h * 256 + 128], cqn[3 + c][:, :n], c == 0, c == 1)
                s.copy('act' if h % 2 == 0 else 'dve', kn_all[b][:, h, :n], p[:, :n])
            s.dma_out('sp', s.d_KN.rearrange("h p t -> p h t")[:, :, t0:t0 + n], kn_all[b][:, :, :n])
            if 'v' in SK_:
                continue
            for sub in range(n // 128):
                vtt = vt[sub % 2]
                for cg in range(2):
                    p = s.ps[pc % 4]
                    pc += 1
                    for c in range(2):
                        rhs = Wkv[c].re("p (h x) -> p h x", h=8)[:, 4 * cg:4 * cg + 4, 128:256]
                        s.mm(p.re("p (h x) -> p h x", h=4), cqn[3 + c][:, sub * 128:(sub + 1) * 128], rhs,
                             c == 0, c == 1)
                    s.copy('act' if cg == 0 else 'dve', vtt[:, cg * 512:(cg + 1) * 512], p)
                s.dma_out('sp', s.d_V[t0 + sub * 128:t0 + (sub + 1) * 128, :], vtt)
        s.phase_end()

    def mla_attn(self, i, with_ctx):
        s = self
        A = s.arena
        T_ = s.T
        NKT = T_ // 128
        KR = A.bf16(T_)
        s.memset('pool', KR[64:128, :], 0.0)
        s.dma_in('sp', KR[0:64, :], s.d_KR)
        KN = [A.bf16(T_) for _ in range(2)]
        V = [A.bf16(T_).re("p (k v) -> p k v", v=128) for _ in range(2)]
        QN = [A.bf16(512) for _ in range(2)]
        QR = [A.bf16(512) for _ in range(2)]
        for qq in QR:
            s.memset('pool', qq[64:128, :], 0.0)
        P = [A.bf16(512) for _ in range(6)]
        accD = [A.f32(512) for _ in range(2)]
        accP = [A.f32(512) for _ in range(2)]
        rl = [A.f32(512) for _ in range(2)]
        oT = [A.bf16(512) for _ in range(2)]
        tiles = s.tiles(with_ctx)
        jobs = [(h, ti) for h in range(8) for ti in range(len(tiles))]
        bg = []
        if s.bg_host is not None and s.bg_host[0] == i:
            bg = s.mod_bg_steps(s.bg_host[1], A)

        def loadkv(h):
            s.dma_in('sp', KN[h % 2], s.d_KN[h])
            s.dma_in('act', V[h % 2], s.d_V.rearrange("(k p) c -> p k c", p=128)[:, :, h * 128:(h + 1) * 128])

        def loadq(jn):
            h, ti = jobs[jn]
            t0, n, w = tiles[ti]
            s.dma_in('sp', QN[jn % 2][:, :n], s.d_QN[h, :, t0:t0 + n])
            s.dma_in('sp', QR[jn % 2][0:64, :n], s.d_QR[h, :, t0:t0 + n])

        loadkv(0)
        loadq(0)
        sc = 0
        for jn, (h, ti) in enumerate(jobs):
            if ti == 0 and h + 1 < 8:
                loadkv(h + 1)
            if jn + 1 < len(jobs):
                loadq(jn + 1)
            t0, n, w = tiles[ti]
            kts = [0, 1] if w else list(range(NKT))
            qn, qr = QN[jn % 2], QR[jn % 2]
            kn, v = KN[h % 2], V[h % 2]
            O, L = s.ps[3 + jn % 2], s.ps[5]
            if jn < len(bg):
                bg[jn]()
            Sb = {}

            def qk(x):
                kt = kts[x]
                S_ = s.ps[(sc + x) % 3]
                s.mm(S_[:, :n], kn[:, kt * 128:(kt + 1) * 128], qn[:, :n], True, False)
                s.mm(S_[:, :n], KR[:, kt * 128:(kt + 1) * 128], qr[:, :n], False, True)
                Pt = P[(sc + x) % 6]
                s.act(Pt[:, :n], S_[:, :n], AF.Exp, scale=MLA_SCALE)
                Sb[x] = Pt

            def pv(x):
                kt = kts[x]
                Pt = Sb.pop(x)
                s.mm(O[:, :n], v[:, kt, :], Pt[:, :n], x == 0, x == len(kts) - 1)
                eng, acc, first = 'dve', accD[jn % 2], x == 0
                if first:
                    s.copy(eng, acc[:, :n], Pt[:, :n])
                else:
                    s.tt(eng, acc[:, :n], acc[:, :n], Pt[:, :n], ALU.add)

            SK = 2
            for x in range(len(kts) + SK):
                if x < len(kts):
                    qk(x)
                if x - SK >= 0:
                    pv(x - SK)
            sc += len(kts)
            twoacc = False
            s.mm(L[:, :n], s.onesf, accD[jn % 2][:, :n], True, not twoacc)
            if twoacc:
                s.mm(L[:, :n], s.onesf, accP[jn % 2][:, :n], False, True)
            s.recip(rl[jn % 2][:, :n], L[:, :n])
            s.tt('dve', oT[jn % 2][:, :n], O[:, :n], rl[jn % 2][:, :n], ALU.mult)
            s.dma_out('sp', s.d_OT[h * 128:(h + 1) * 128, t0:t0 + n], oT[jn % 2][:, :n])
        for st in bg[len(jobs):]:
            st()
        s.phase_end()

    def outproj(self, wdram, i, with_ctx, pidx=0, nxt=None):
        s = self
        A = s.arena
        s.ffn_wsets()
        Wo = [A.bf16(1024) for _ in range(8)]
        for k in range(8):
            s.wload(Wo[k], wdram[k * 128:(k + 1) * 128, :])
        if nxt is not None:
            s.ffn_wload(nxt, pidx % 2)
        hb = [A.f32(8 * 512).re("p (c t) -> p c t", c=8) for _ in range(2)]
        ob = [A.bf16(8 * 512).re("p (c t) -> p c t", c=8) for _ in range(2)]
        tiles = s.tiles(with_ctx)

        def load(ti):
            t0, n, w = tiles[ti]
            s.dma_in('sp', hb[ti % 2][:, :, :n], s.hview(s.d_H, t0, n))
            s.dma_in('sp', ob[ti % 2][:, :, :n], s.hview(s.d_OT, t0, n))

        load(0)
        for ti, (t0, n, w) in enumerate(tiles):
            if ti + 1 < len(tiles):
                load(ti + 1)
            h3, o3 = hb[ti % 2], ob[ti % 2]
            for dch in range(8):
                py = s.ps[dch % 4]
                for k in range(8):
                    s.mm(py[:, :n], Wo[k][:, dch * 128:(dch + 1) * 128], o3[:, k, :n], k == 0, k == 7)
                s.stt(h3[:, dch, :n], py[:, :n], s.hgc(i, 1, dch, w), h3[:, dch, :n], ALU.mult, ALU.add)
            s.dma_out('sp', s.hview(s.d_H, t0, n), h3[:, :, :n])
        s.phase_end()

    def mla_layer(self, i, with_ctx, pidx=0, nxt=None):
        st = self.cfg.get('mla_stop', 3)
        self.mla_proj(i, self.d_H)
        if st >= 2:
            self.mla_attn(i, with_ctx)
        if st >= 3:
            self.outproj(self.d_mla_o[i // 2], i, with_ctx, pidx, nxt)

    def build(self):
        s = self
        nc = s.nc
        T_ = s.T
        dp = s.depth
        s.d_xT = s.din("xT", [D, T_])
        s.d_cols = s.din("cols", [128, NCOLS])
        s.d_consts = s.din("consts", [128, 384])
        s.d_mod_w = s.din("mod_w", [4, D, NMOD * D])
        s.d_wg = s.din("ffn_w_gate", [4, 2, D, 2816])
        s.d_wu = s.din("ffn_w_up", [4, 2, D, 2816])
        s.d_wd = s.din("ffn_w_down", [4, 2, 2816, D])
        s.d_out = nc.dram_tensor("outT", [D, s.TL], F32, kind="ExternalOutput").ap()
        s.d_rope = s.din("rope", [2, 64, T_])
        s.d_mla_down = s.din("mla_w_down", [2, D, 704])
        s.d_mla_uq = s.din("mla_w_uq", [2, 384, 1536])
        s.d_mla_ukv = s.din("mla_w_ukv", [2, 256, 2048])
        s.d_mla_o = s.din("mla_w_o", [2, D, D])
        s.d_KR = s.dscr("KR", [64, T_], BF16)
        s.d_QN = s.dscr("QN", [8, 128, T_], BF16)
        s.d_QR = s.dscr("QR", [8, 64, T_], BF16)
        s.d_KN = s.dscr("KN", [8, 128, T_], BF16)
        s.d_V = s.dscr("V", [T_, D], BF16)
        s.d_OT = s.dscr("OT", [D, T_], BF16)
        s.d_hg_in = s.din("hg_w_in", [2, D, 5120])
        s.d_hg_out = s.din("hg_w_out", [2, D, D])
        s.d_GQ = s.dscr("GQ", [2, 8, 128, T_], BF16)
        s.d_GK = s.dscr("GK", [2, 8, 128, T_], BF16)
        s.d_GE = s.dscr("GE", [2, 8, 128, T_], BF16)
        s.d_GD = s.dscr("GD", [2, 8, 128, T_ // 32], F32)
        s.d_GS = s.dscr("GS", [D, T_], F32)
        s.d_OF = s.dscr("OF", [D, T_], F32)
        s.d_OB = s.dscr("OB", [D, T_], F32)
        s.d_H = s.dscr("H", [D, T_], F32)
        s.d_AT = s.dscr("AT", [D, T_], BF16)
        with ExitStack() as es:
            arena = es.enter_context(nc.sbuf_tensor("sb_arena", [128, 49 * 1024], F32))
            s.arena = Arena(arena[:, :])
            s.cols = T(es.enter_context(nc.sbuf_tensor("sb_cols", [128, NCOLS], F32))[:, :])
            s.MODC = T(es.enter_context(nc.sbuf_tensor("sb_modc", [128, 4 * NMOD * NCH * 2], F32))[:, :])
            s.GM = T(es.enter_context(nc.sbuf_tensor("sb_gm", [128, 4 * 3 * NCH * 2], F32))[:, :])
            s.HG = T(es.enter_context(nc.sbuf_tensor("sb_hg", [128, 4 * 3 * NCH * 2], F32))[:, :])
            s.LB = T(es.enter_context(nc.sbuf_tensor("sb_lb", [128, 32], F32))[:, :])
            s.OML = T(es.enter_context(nc.sbuf_tensor("sb_oml", [128, 32], F32))[:, :])
            s.cst = T(es.enter_context(nc.sbuf_tensor("sb_cst", [128, 384], BF16))[:, :])
            s.ones = T(es.enter_context(nc.sbuf_tensor("sb_ones", [128, 128], BF16))[:, :])
            epst = es.enter_context(nc.sbuf_tensor("sb_epsc", [128, 1], F32))
            s.epsT = T(epst[:, :])
            s.ident = s.cst[:, 0:128]
            s.maskF = s.cst[:, 128:256]
            s.maskB = s.cst[:, 256:384]
            psum = es.enter_context(nc.psum_tensor("ps_psum", [128, 4096], F32))
            s.ps = [T(psum[:, b * 512:(b + 1) * 512]) for b in range(8)]
            s.psum_all = psum[:, :]
            s.memset('pool', s.epsT, RMS_EPS)
            s.epsc = s.epsT
            s.identf = T(es.enter_context(nc.sbuf_tensor("sb_identf", [128, 128], F32))[:, :])
            s.scbf = T(es.enter_context(nc.sbuf_tensor("sb_scbf", [128, 16], BF16))[:, :]).re("p (c w) -> p c w", w=2)
            s.onesf = T(es.enter_context(nc.sbuf_tensor("sb_onesf", [128, 128], F32))[:, :])
            s.memset('pool', s.onesf, 1.0)
            s.onec = T(es.enter_context(nc.sbuf_tensor("sb_onec", [128, 1], F32))[:, :])
            s.memset('pool', s.onec, 1.0)

            s.program()

            s.S.finalize()
            csem = {e: [es.enter_context(nc.semaphore("c_%s_%d" % (e, k))) for k in range(s.S.nsem[e])]
                    for e in ('pe', 'act', 'dve', 'pool')}
            dsem = {q: [es.enter_context(nc.semaphore("d_%s_%d" % (q, k))) for k in range(NSLOT)] for q in DMAQ}
            block = es.enter_context(nc.Block())

            @block.tensor
            def _(e):
                s.S.emit_engine('pe', e, csem, dsem)

            @block.scalar
            def _(e):
                s.S.emit_engine('act', e, csem, dsem)

            @block.vector
            def _(e):
                s.S.emit_engine('dve', e, csem, dsem)

            @block.gpsimd
            def _(e):
                s.S.emit_engine('pool', e, csem, dsem)

            @block.sync
            def _(e):
                s.S.emit_engine('sp', e, csem, dsem)
        return nc

    def program(self):
        s = self
        s.phase_consts()
        mixes0 = s.cfg.get('mix', ['hgrn', 'mla', 'hgrn', 'mla'])
        s.bg_host = None
        if s.cfg.get('mod_bg', True) and s.depth == 4 and mixes0[1] == 'mla':
            s.phase_mod([0, 1])
            s.bg_host = (1, [2, 3])
        else:
            s.phase_mod(list(range(s.depth)))
        ph = []
        src = s.d_xT
        mixes = s.cfg.get('mix', ['hgrn', 'mla', 'hgrn', 'mla'])
        for i in range(s.depth):
            ph += s.ffn_descs(i, 0, src, s.d_H)
            src = s.d_H
            last = (i == s.depth - 1) and not s.cfg.get('always_ctx', False)
            if mixes[i] is not None:
                ph.append((mixes[i], (i, not last)))
            ph += s.ffn_descs(i, 1, s.d_H, s.d_H, with_ctx=not last)
        pidx = 0
        prefetched = False
        for x, (kind, d) in enumerate(ph):
            nxt = ph[x + 1][1] if (x + 1 < len(ph) and ph[x + 1][0] == 'ffn') else None
            if not s.cfg.get('prefetch', True):
                nxt = None
            if kind == 'ffn':
                s.ffn_pass(d, pidx, prefetched, nxt)
                pidx += 1
            elif kind == 'mla':
                s.mla_layer(d[0], d[1], pidx, nxt)
            elif kind == 'hgrn':
                s.hgrn_layer(d[0], d[1], pidx, nxt)
            prefetched = nxt is not None
        s.phase_final(s.d_H)


def colify(v):
    v = np.asarray(v, np.float32)
    return np.ascontiguousarray(v.reshape(-1, 128).T)


def host_consts():
    ident = np.eye(128, dtype=np.float32)
    s_ = np.arange(128)[:, None]
    t_ = np.arange(128)[None, :]
    same = (s_ // 32) == (t_ // 32)
    maskF = (same & (s_ <= t_)).astype(np.float32)
    maskB = (same & (s_ >= t_)).astype(np.float32)
    return np.concatenate([ident, maskF, maskB], axis=1)


def rope_tables(TL):
    rows = TL // 64
    row = np.repeat(np.arange(rows), 64).astype(np.float32)
    colp = np.tile(np.arange(64), rows).astype(np.float32)
    inv = (1.0 / (np.float32(10000.0) ** (np.arange(0, 32, 2, dtype=np.float32) / np.float32(32)))).astype(np.float32)
    out = np.zeros((2, 64, CTX + TL), np.float32)
    out[0, :, :CTX] = 1.0
    for ax, pos in enumerate((row, colp)):
        ang = (pos[None, :] * inv[:, None]).astype(np.float32)
        for hf in range(2):
            r0 = ax * 32 + hf * 16
            out[0, r0:r0 + 16, CTX:] = np.cos(ang)
            out[1, r0:r0 + 16, CTX:] = np.sin(ang)
    return out


def pack_cols(inp, b):
    cols = np.zeros((128, NCOLS), np.float32)
    cols[:, COL_C:COL_C + 8] = colify(inp['c'][b])
    cols[:, COL_C + 8:COL_C + 16] = colify(inp['c_ctx'])
    nl = inp['mod_b'].shape[0]
    for i in range(nl):
        cols[:, COL_MODB + i * 72:COL_MODB + (i + 1) * 72] = colify(inp['mod_b'][i])
        for k in range(3):
            cols[:, COL_NG + (i * 3 + k) * 8:COL_NG + (i * 3 + k + 1) * 8] = colify(inp['norm_g'][i, k])
    cols[:, COL_FG:COL_FG + 8] = colify(inp['final_g'])
    for j in range(inp['hg_gn'].shape[0]):
        cols[:, COL_GN + j] = inp['hg_gn'][j]
        for d in range(2):
            cols[:, COL_LB + (j * 2 + d) * 8:COL_LB + (j * 2 + d + 1) * 8] = colify(inp['hg_lb_logits'][j, d])
        cols[:, COL_QN + j * 3:COL_QN + j * 3 + 3] = colify(inp['mla_q_norm'][j])
        cols[:, COL_KVN + j * 2:COL_KVN + j * 2 + 2] = colify(inp['mla_kv_norm'][j])
    return cols


_CACHE = {}


def run(inp, cfg, n_cores):
    inp = {k: np.asarray(v) for k, v in inp.items()}
    key = repr(sorted(cfg.items()))
    kb = K(cfg)
    nc = kb.build()
    consts = host_consts()
    rope = rope_tables(kb.TL)
    in_maps = []
    for b in range(n_cores):
        xT = np.ascontiguousarray(np.concatenate([inp['ctx'][b], inp['x'][b]], axis=0).T.astype(np.float32))
        m = {"xT": xT, "cols": pack_cols(inp, b), "consts": consts,
             "mod_w": inp['mod_w'], "ffn_w_gate": inp['ffn_w_gate'], "ffn_w_up": inp['ffn_w_up'],
             "ffn_w_down": inp['ffn_w_down'], "rope": rope,
             "mla_w_down": inp['mla_w_down'], "mla_w_uq": inp['mla_w_uq'], "mla_w_ukv": inp['mla_w_ukv'],
             "mla_w_o": inp['mla_w_o'], "hg_w_in": inp['hg_w_in'], "hg_w_out": inp['hg_w_out']}
        in_maps.append(m)
    res = run_bass_kernel_spmd(nc, in_maps, core_ids=list(range(n_cores)))
    return res.results


def kernel(**inputs):
    res = run(inputs, {}, 8)
    out = np.stack([np.ascontiguousarray(r["outT"].T) for r in res], axis=0)
    return out.astype(np.float32)
```
